# Optimizing a Trainium2 kernel written in Bass

```python
import math
import jax, jax.numpy as jnp
from jax import lax
import numpy as np

D_MODEL = 1024
BATCH = 8
SEQ = 4096
DEPTH = 1

MIX_WIDTH = D_MODEL
ATTN_WIDTH = MIX_WIDTH // 2
LRU_WIDTH = MIX_WIDTH - ATTN_WIDTH
HEAD_DIM = 64
N_ATTN_HEADS = ATTN_WIDTH // HEAD_DIM
LRU_BLOCK = 64
N_LRU_BLOCKS = LRU_WIDTH // LRU_BLOCK
CONV_WIDTH = 4
LRU_C = 8.0
ROPE_THETA = 500000.0
ROPE_DIMS = HEAD_DIM // 4
DILATED_PATTERNS = ((128, 1), (512, 4), (2048, 16))
D_FF = 4 * D_MODEL
IN_COLS = 3 * ATTN_WIDTH + 2 * LRU_WIDTH
EPS = 1e-6
NEG = -1e30

kernel_name = "hybrid_rglru_dilated_attn_adaln_block"


def rms_norm(x, g):
    xf = x.astype(jnp.float32)
    y = xf * lax.rsqrt(jnp.mean(xf * xf, axis=-1, keepdims=True) + EPS)
    return (y * g.astype(jnp.float32)).astype(x.dtype)


def partial_rotary(t, positions):
    half = ROPE_DIMS // 2
    inv_freq = ROPE_THETA ** (-jnp.arange(half, dtype=jnp.float32) / half)
    ang = positions.astype(jnp.float32)[..., None] * inv_freq
    cos = jnp.cos(ang)[:, :, None, :]
    sin = jnp.sin(ang)[:, :, None, :]
    x1 = t[..., :half]
    x2 = t[..., half:ROPE_DIMS]
    return jnp.concatenate([x1 * cos - x2 * sin, x2 * cos + x1 * sin, t[..., ROPE_DIMS:]], axis=-1)


def dilated_band_attention(q, k, v, dilation, steps):
    b, s, h, dh = q.shape
    n = s // dilation
    blk = steps
    nb = -(-n // blk)
    n_pad = nb * blk

    def to_sub(t):
        return t.reshape(b, n, dilation, h, dh).transpose(0, 2, 3, 1, 4)

    qs, ks, vs = to_sub(q), to_sub(k), to_sub(v)
    qb = jnp.pad(qs, ((0, 0), (0, 0), (0, 0), (0, n_pad - n), (0, 0))).reshape(b, dilation, h, nb, blk, dh)

    def band(t):
        t = jnp.pad(t, ((0, 0), (0, 0), (0, 0), (blk, n_pad - n + blk), (0, 0)))
        t = t.reshape(b, dilation, h, nb + 2, blk, dh)
        return jnp.concatenate([t[:, :, :, :-2], t[:, :, :, 1:-1], t[:, :, :, 2:]], axis=4)

    kb, vb = band(ks), band(vs)
    qi = jnp.arange(nb)[:, None] * blk + jnp.arange(blk)[None, :]
    ki = jnp.arange(nb)[:, None] * blk - blk + jnp.arange(3 * blk)[None, :]
    rel = ki[:, None, :] - qi[:, :, None]
    mask = (jnp.abs(rel) <= steps) & (ki[:, None, :] >= 0) & (ki[:, None, :] < n)

    scores = jnp.einsum('bdhnqe,bdhnke->bdhnqk', qb, kb) * (dh ** -0.5)
    scores = jnp.where(mask, scores, NEG)
    m = jnp.max(scores, axis=-1, keepdims=True)
    p = jnp.where(mask, jnp.exp(scores - m), 0.0)
    l = jnp.sum(p, axis=-1, keepdims=True)
    o = jnp.einsum('bdhnqk,bdhnke->bdhnqe', p, vb) / l
    lse = (m + jnp.log(l))[..., 0]

    o = o.reshape(b, dilation, h, n_pad, dh)[:, :, :, :n].transpose(0, 3, 1, 2, 4).reshape(b, s, h, dh)
    lse = lse.reshape(b, dilation, h, n_pad)[:, :, :, :n].transpose(0, 3, 1, 2).reshape(b, s, h)
    return o, lse


def dilated_mixture_attention(q, k, v, positions):
    b, s, _ = q.shape
    shp = (b, s, N_ATTN_HEADS, HEAD_DIM)
    qh = partial_rotary(q.astype(jnp.float32).reshape(shp), positions)
    kh = partial_rotary(k.astype(jnp.float32).reshape(shp), positions)
    vh = v.astype(jnp.float32).reshape(shp)
    outs, lses = [], []
    for window, dilation in DILATED_PATTERNS:
        o, lse = dilated_band_attention(qh, kh, vh, dilation, window // (2 * dilation))
        outs.append(o)
        lses.append(lse)
    w = jax.nn.softmax(jnp.stack(lses, axis=0), axis=0)
    o = jnp.sum(w[..., None] * jnp.stack(outs, axis=0), axis=0)
    return o.reshape(b, s, ATTN_WIDTH)


def centred_depthwise_conv(x, w, bias):
    left = CONV_WIDTH // 2
    out = lax.conv_general_dilated(
        x, w.astype(jnp.float32)[:, None, :], window_strides=(1,),
        padding=[(left, CONV_WIDTH - 1 - left)],
        dimension_numbers=('NWC', 'WIO', 'NWC'), feature_group_count=x.shape[-1])
    return out + bias.astype(jnp.float32)


def rg_lru_scan(xc, w_a, b_a, w_x, b_x, lam):
    b, s, wd = xc.shape
    xh = xc.reshape(b, s, N_LRU_BLOCKS, LRU_BLOCK)
    r = jax.nn.sigmoid(jnp.einsum('bsni,nij->bsnj', xh, w_a.astype(jnp.float32)).reshape(b, s, wd)
                       + b_a.astype(jnp.float32))
    i = jax.nn.sigmoid(jnp.einsum('bsni,nij->bsnj', xh, w_x.astype(jnp.float32)).reshape(b, s, wd)
                       + b_x.astype(jnp.float32))
    log_a = -LRU_C * r * jax.nn.softplus(-lam.astype(jnp.float32))
    a = jnp.exp(log_a)
    u = jnp.sqrt(-jnp.expm1(2.0 * log_a)) * (i * xc)

    def combine(e1, e2):
        a1, b1 = e1
        a2, b2 = e2
        return a1 * a2, a2 * b1 + b2

    _, h = lax.associative_scan(combine, (a, u), axis=1)
    return h


def bidirectional_rg_lru(xr, gr, conv_w, conv_b, wa, ba, wx, bx, lam):
    xc = centred_depthwise_conv(xr.astype(jnp.float32), conv_w, conv_b)
    h_f = rg_lru_scan(xc, wa[0], ba[0], wx[0], bx[0], lam[0])
    h_b = jnp.flip(rg_lru_scan(jnp.flip(xc, axis=1), wa[1], ba[1], wx[1], bx[1], lam[1]), axis=1)
    return (h_f + h_b) * jax.nn.gelu(gr.astype(jnp.float32), approximate=True)


def setup_inputs(seed: int = 0) -> dict:
    key = jax.random.key(seed)
    ks = jax.random.split(key, 24)
    f32 = jnp.float32

    def nrm(k, shape, scale):
        return jax.random.normal(k, shape, f32) * scale

    x = nrm(ks[0], (BATCH, SEQ, D_MODEL), 1.0)
    c = nrm(ks[1], (BATCH, D_MODEL), 1.0)
    offsets = jax.random.randint(ks[2], (BATCH, 1), 0, 1024, dtype=jnp.int32)
    positions = (offsets + jnp.arange(SEQ, dtype=jnp.int32)[None, :]).astype(jnp.int32)
    w_ada = nrm(ks[3], (DEPTH, D_MODEL, 6 * D_MODEL), 0.5 * D_MODEL ** -0.5)
    b_ada = nrm(ks[4], (DEPTH, 6 * D_MODEL), 0.02)
    norm1_g = 1.0 + nrm(ks[5], (DEPTH, D_MODEL), 0.02)
    w_in = nrm(ks[6], (DEPTH, D_MODEL, IN_COLS), D_MODEL ** -0.5)
    conv_w = nrm(ks[7], (DEPTH, CONV_WIDTH, LRU_WIDTH), CONV_WIDTH ** -0.5)
    conv_b = nrm(ks[8], (DEPTH, LRU_WIDTH), 0.02)
    lru_wa = nrm(ks[9], (DEPTH, 2, N_LRU_BLOCKS, LRU_BLOCK, LRU_BLOCK), LRU_BLOCK ** -0.5)
    lru_ba = nrm(ks[10], (DEPTH, 2, LRU_WIDTH), 0.02)
    lru_wx = nrm(ks[11], (DEPTH, 2, N_LRU_BLOCKS, LRU_BLOCK, LRU_BLOCK), LRU_BLOCK ** -0.5)
    lru_bx = nrm(ks[12], (DEPTH, 2, LRU_WIDTH), 0.02)
    u = jax.random.uniform(ks[13], (DEPTH, 2, LRU_WIDTH), f32, 0.9, 0.999)
    a0 = u ** (1.0 / LRU_C)
    lru_lam = jnp.log(a0) - jnp.log1p(-a0)
    attn_out_g = 1.0 + nrm(ks[14], (DEPTH, ATTN_WIDTH), 0.02)
    lru_out_g = 1.0 + nrm(ks[15], (DEPTH, LRU_WIDTH), 0.02)
    w_out = nrm(ks[16], (DEPTH, MIX_WIDTH, D_MODEL), MIX_WIDTH ** -0.5)
    norm2_g = 1.0 + nrm(ks[17], (DEPTH, D_MODEL), 0.02)
    w_ff1 = nrm(ks[18], (DEPTH, D_MODEL, D_FF), D_MODEL ** -0.5)
    w_ff2 = nrm(ks[19], (DEPTH, D_FF, D_MODEL), D_FF ** -0.5)
    final_g = 1.0 + nrm(ks[20], (D_MODEL,), 0.02)
    return {"x": x, "c": c, "positions": positions, "w_ada": w_ada, "b_ada": b_ada,
            "norm1_g": norm1_g, "w_in": w_in, "conv_w": conv_w, "conv_b": conv_b,
            "lru_wa": lru_wa, "lru_ba": lru_ba, "lru_wx": lru_wx, "lru_bx": lru_bx,
            "lru_lam": lru_lam, "attn_out_g": attn_out_g, "lru_out_g": lru_out_g,
            "w_out": w_out, "norm2_g": norm2_g, "w_ff1": w_ff1, "w_ff2": w_ff2,
            "final_g": final_g}


def reference(x, c, positions, w_ada, b_ada, norm1_g, w_in, conv_w, conv_b, lru_wa, lru_ba,
              lru_wx, lru_bx, lru_lam, attn_out_g, lru_out_g, w_out, norm2_g, w_ff1, w_ff2,
              final_g):
    split_cols = [ATTN_WIDTH, 2 * ATTN_WIDTH, 3 * ATTN_WIDTH, 3 * ATTN_WIDTH + LRU_WIDTH]
    for l in range(DEPTH):
        mod = jax.nn.silu(c) @ w_ada[l] + b_ada[l]
        shift1, scale1, gate1, shift2, scale2, gate2 = [t[:, None, :] for t in jnp.split(mod, 6, axis=-1)]

        h = rms_norm(x, norm1_g[l]) * (1.0 + scale1) + shift1
        proj = h @ w_in[l]
        q, k, v, xr, gr = jnp.split(proj, split_cols, axis=-1)
        attn = dilated_mixture_attention(q, k, v, positions)
        rec = bidirectional_rg_lru(xr, gr, conv_w[l], conv_b[l], lru_wa[l], lru_ba[l],
                                   lru_wx[l], lru_bx[l], lru_lam[l])
        mixed = jnp.concatenate([rms_norm(attn, attn_out_g[l]), rms_norm(rec, lru_out_g[l])], axis=-1)
        x = x + gate1 * (mixed.astype(x.dtype) @ w_out[l])

        h2 = rms_norm(x, norm2_g[l]) * (1.0 + scale2) + shift2
        ff = jnp.square(jax.nn.relu(h2 @ w_ff1[l])) @ w_ff2[l]
        x = x + gate2 * ff
    return rms_norm(x, final_g)
```

```python
import math
from contextlib import ExitStack

import numpy as np
import concourse.bass as bass
import concourse.mybir as mybir
from concourse.bass_utils import run_bass_kernel_spmd

F32 = mybir.dt.float32
BF16 = mybir.dt.bfloat16
I32 = mybir.dt.int32
AF = mybir.ActivationFunctionType
ALU = mybir.AluOpType

SEQ = 4096
DM = 1024
NT = SEQ // 128
EPS = 1e-6
PI = math.pi
TWO_PI = 2.0 * math.pi
PI_C = 3.1415925
DILS = (1, 4, 16)


class Sched:
    ENGS = ("pe", "act", "dve", "pool", "sp")

    def __init__(self, nc, stack):
        self.nc = nc
        self.stack = stack
        self.ops = {e: [] for e in self.ENGS}
        self.clock_sem = {e: stack.enter_context(nc.semaphore("clk_" + e)) for e in ("pe", "act", "dve", "pool")}
        self.clock = {e: 0 for e in self.clock_sem}
        self.seen = {e: {} for e in self.ENGS}
        self.lastw = {}
        self.readers = {}
        self.dma_sems = {}

    def dma_sem(self, key):
        if key not in self.dma_sems:
            self.dma_sems[key] = [self.stack.enter_context(self.nc.semaphore("dma_%d" % len(self.dma_sems))), 0]
        return self.dma_sems[key]

    def _deps(self, eng, reads, writes, waiter=None):
        waiter = waiter or eng
        need = {}

        def add(ev):
            if ev is None:
                return
            if need.get(ev[0], 0) < ev[1]:
                need[ev[0]] = ev[1]
        for t in reads:
            add(self.lastw.get(t))
        for t in writes:
            w = self.lastw.get(t)
            if w is not None and (w[2] != eng or eng in ("act", "dve", "pool")):
                add(w)
            for r in self.readers.get(t, ()):
                if r[2] != eng:
                    add(r)
        waits = []
        for sem, val in need.items():
            if self.seen[waiter].get(sem, 0) < val:
                self.seen[waiter][sem] = val
                waits.append((sem, val))
        return waits

    def _commit(self, ev, reads, writes):
        for t in reads:
            self.readers.setdefault(t, []).append(ev)
        for t in writes:
            self.lastw[t] = ev
            self.readers[t] = []

    def op(self, eng, fn, reads=(), writes=()):
        waits = self._deps(eng, reads, writes)
        self.clock[eng] += 1
        sem = self.clock_sem[eng]
        ev = (sem, self.clock[eng], eng)
        self._commit(ev, reads, writes)
        self.ops[eng].append((waits, fn, sem, 1))
        return ev

    def dma(self, queue, fn, key, reads=(), writes=()):
        self.ndma = getattr(self, "ndma", 0) + 1
        waits = self._deps("dma#%d" % self.ndma, reads, writes, waiter=queue)
        s = self.dma_sem(key)
        s[1] += 16
        ev = (s[0], s[1], "dma")
        self._commit(ev, reads, writes)
        self.ops[queue].append((waits, fn, s[0], 16))
        return ev

    def dma_group(self, queue, items, key):
        ev = None
        toks = []
        for fn, reads, writes in items:
            ev = self.dma(queue, fn, key, reads, writes)
            toks += list(writes)
        for t in toks:
            self.lastw[t] = ev
        return ev

    def alias(self, new_tokens, old_tokens):
        evs = []
        for t in old_tokens:
            if self.lastw.get(t) is not None:
                evs.append(self.lastw[t])
            evs += list(self.readers.get(t, ()))
        for t in new_tokens:
            self.lastw[t] = None
            self.readers[t] = [(e[0], e[1], "alias") for e in evs]

    def wait_all(self, eng, evs):
        waits = []
        for ev in evs:
            if self.seen[eng].get(ev[0], 0) < ev[1]:
                self.seen[eng][ev[0]] = ev[1]
                waits.append((ev[0], ev[1]))
        if waits:
            self.ops[eng].append((waits, None, None, 0))

    def barrier(self):
        evs = [(self.clock_sem[e], self.clock[e]) for e in self.clock_sem if self.clock[e] > 0]
        evs += [(s[0], s[1]) for s in self.dma_sems.values() if s[1] > 0]
        for e in self.ENGS:
            self.wait_all(e, evs)

    def emit(self, block):
        def run(name):
            def body(e):
                for waits, fn, sem, inc in self.ops[name]:
                    for (s, v) in waits:
                        e.wait_ge(s, v)
                    if fn is not None:
                        fn(e).then_inc(sem, inc)
            return body
        block.tensor(run("pe"))
        block.scalar(run("act"))
        block.vector(run("dve"))
        block.gpsimd(run("pool"))
        block.sync(run("sp"))


def build_program(dbg=None):
    nc = bass.Bass("TRN2", target_bir_lowering=False)

    def din(name, shape, dt=F32):
        return nc.dram_tensor(name, list(shape), dt, kind="ExternalInput").ap()

    x = din("x", [SEQ, DM])
    crow = din("crow", [8, 128])
    pos = din("pos", [1, SEQ], I32)
    w_ada = din("w_ada", [DM, 6 * DM])
    b_ada = din("b_ada", [48, 128])
    norm1_g = din("norm1_g", [8, 128])
    norm2_g = din("norm2_g", [8, 128])
    final_g = din("final_g", [8, 128])
    w_in = din("w_in", [DM, 2560])
    conv_w = din("conv_w", [16, 128])
    conv_b = din("conv_b", [4, 128])
    lru_wa = din("lru_wa", [16, 64, 64])
    lru_wx = din("lru_wx", [16, 64, 64])
    lru_ba = din("lru_ba", [8, 128])
    lru_bx = din("lru_bx", [8, 128])
    lru_lam = din("lru_lam", [8, 128])
    attn_out_g = din("attn_out_g", [4, 128])
    lru_out_g = din("lru_out_g", [4, 128])
    w_out = din("w_out", [DM, DM])
    w_ff1 = din("w_ff1", [DM, 4 * DM])
    w_ff2 = din("w_ff2", [4 * DM, DM])
    cst = din("cst", [128, 512])
    y = nc.dram_tensor("y", [SEQ, DM], F32, kind="ExternalOutput").ap()
    mixs = nc.dram_tensor("mixs", [8, 128, SEQ], BF16, kind="ExternalOutput" if dbg else "Internal").ap()
    if dbg:
        dbgf = nc.dram_tensor("dbgf", [128, 16384], F32, kind="ExternalOutput").ap()
        dbgb = nc.dram_tensor("dbgb", [128, 65536], BF16, kind="ExternalOutput").ap()

    w_in_v = w_in.rearrange("(k p) n -> p k n", p=128)
    w_ada_v = w_ada.rearrange("(k p) n -> p k n", p=128)
    w_out_v = w_out.rearrange("(k p) n -> p k n", p=128)
    w_ff1_v = w_ff1.rearrange("(k p) n -> p k n", p=128)
    w_ff2_v = w_ff2.rearrange("(k p) n -> p k n", p=128)
    mixs_v = mixs.rearrange("k p t -> p k t")

    with ExitStack() as st:
        S = Sched(nc, st)

        def sb(name, shape, dt):
            return st.enter_context(nc.sbuf_tensor(name, list(shape), dt))

        ident_f = sb("ident_f", [128, 128], F32)
        ident_bf = sb("ident_bf", [128, 128], BF16)
        ones_f = sb("ones_f", [128, 128], F32)
        ones_bf = sb("ones_bf", [128, 128], BF16)
        cstf = sb("cstf", [128, 512], F32)
        maskb = sb("maskb", [128, 256], BF16)
        psw = sb("psw", [128, 128], BF16)
        pcolA = sb("pcolA", [128, 80], F32)
        pcolB = sb("pcolB", [128, 64], F32)
        silu_c = sb("silu_c", [128, 8], F32)
        modc = sb("modc", [128, 48], F32)
        gs1 = sb("gs1", [128, 8], F32)
        gs2 = sb("gs2", [128, 8], F32)
        cs = sb("cs", [128, 8], F32)
        cst1 = sb("cst1", [128, 8], F32)
        cst2 = sb("cst2", [128, 8], F32)
        ssq1 = sb("ssq1", [128, NT], F32)
        ms1 = sb("ms1", [128, NT], F32)
        rstd1 = sb("rstd1", [128, NT], F32)
        ssq_a = sb("ssq_a", [128, NT], F32)
        ssq_l = sb("ssq_l", [128, NT], F32)
        rstd_a = sb("rstd_a", [128, NT], F32)
        rstd_l = sb("rstd_l", [128, NT], F32)
        colt = sb("colt", [128, 8], F32)
        neghalf32 = sb("neghalf32", [128, NT], F32)

        R1 = sb("R1", [128, 16384], F32)
        R2 = sb("R2", [128, 16384], F32)
        R3 = sb("R3", [128, 8192], F32)
        R4 = sb("R4", [128, 7168], F32)
        prowA = R4[0:80, 3584:3712]
        prowB = R4[0:64, 3712:3840]
        R5 = sb("R5", [128, 3072], F32)
        fg_bc = R5[:, 1216:2240]
        wablk = R5[:, 0:1024].bitcast(BF16).rearrange("p (i n) -> p i n", i=16)

        pb = [st.enter_context(nc.psum_tensor("pb%d" % i, [128, 512], F32)) for i in range(8)]

        def bf(ap):
            return ap.bitcast(BF16)

        MASK = cstf[:, 0:256]
        SEL = cstf[0:65, 384:448]
        INVF = cstf[:, 448:449]
        SGN = cstf[:, 449:450]
        NEGHALF = cstf[:, 450:451]

        def ACT(out, in_, func, reads, writes, bias=None, scale=None, accum=None):
            kw = {}
            if bias is not None:
                kw["bias"] = bias
            if scale is not None:
                kw["scale"] = scale
            if accum is not None:
                kw["accum_out"] = accum
            return S.op("act", lambda e: e.activation(out=out, in_=in_, func=func, **kw), reads, writes)

        def TS(eng, out, in0, s1, s2, op0, op1, reads, writes):
            if op1 is None:
                return S.op(eng, lambda e: e.tensor_scalar(out=out, in0=in0, scalar1=s1, scalar2=None, op0=op0), reads, writes)
            return S.op(eng, lambda e: e.tensor_scalar(out=out, in0=in0, scalar1=s1, scalar2=s2, op0=op0, op1=op1), reads, writes)

        def STT(out, in0, scalar, in1, op0, op1, reads, writes):
            return S.op("dve", lambda e: e.scalar_tensor_tensor(out=out, in0=in0, scalar=scalar, in1=in1, op0=op0, op1=op1), reads, writes)

        def TT(eng, out, in0, in1, op, reads, writes):
            return S.op(eng, lambda e: e.tensor_tensor(out=out, in0=in0, in1=in1, op=op), reads, writes)

        def CP(eng, out, in_, reads, writes):
            return S.op(eng, lambda e: e.tensor_copy(out=out, in_=in_), reads, writes)

        def MM(out, pairs, reads, writes, first_start=True, skip=False):
            def fn(e):
                ins = None
                n = len(pairs)
                for i, (l, r) in enumerate(pairs):
                    ins = e.matmul(out, lhsT=l, rhs=r, start=(first_start and i == 0), stop=(i == n - 1),
                                   skip_group_check=skip)
                return ins
            return S.op("pe", fn, reads, writes)

        def TRS(items, reads, writes):
            def fn(e):
                ins = None
                for o, i in items:
                    ins = e.transpose(out=o, in_=i, identity=ident_bf[:])
                return ins
            return S.op("pe", fn, reads, writes)

        def DMA(queue, out, in_, key, reads, writes):
            return S.dma(queue, lambda e: e.dma_start(out=out, in_=in_), key, reads, writes)

        def finish(dumps=()):
            evs = []
            fo = bo = 0
            for ap, toks in dumps:
                n = ap.shape[-1]
                if ap.dtype == F32:
                    evs.append(DMA("sp", dbgf[:, fo:fo + n], ap, "dbg", toks, []))
                    fo += n
                else:
                    evs.append(DMA("sp", dbgb[:, bo:bo + n], ap, "dbg", toks, []))
                    bo += n
            S.barrier()
            with nc.Block() as block:
                S.emit(block)
            return nc

        S.op("pool", lambda e: e.memset(ident_f[:], 1.0), [], ["ident_f"])
        S.op("pool", lambda e: e.affine_select(out=ident_f[:], in_=ident_f[:], pattern=[[-1, 128]],
                                               compare_op=ALU.is_equal, fill=0.0, base=0, channel_multiplier=1),
             ["ident_f"], ["ident_f"])
        S.op("pool", lambda e: e.memset(ones_f[:], 1.0), [], ["ones_f"])
        S.op("pool", lambda e: e.memset(ones_bf[:], 1.0), [], ["ones_bf"])
        S.op("pool", lambda e: e.memset(prowB, 0.0), [], ["prowB"])
        S.op("pool", lambda e: e.memset(neghalf32[:], -0.5), [], ["neghalf32"])
        CP("pool", ident_bf[:], ident_f[:], ["ident_f"], ["ident_bf"])
        DMA("sp", cstf[:], cst, "cst", [], ["cstf"])
        CP("dve", maskb[:], cstf[:, 0:256], ["cstf"], ["maskb"])
        CP("dve", psw[:], cstf[:, 256:384], ["cstf"], ["psw"])

        items = []
        for (dst, src) in ((prowA[0:48, :], b_ada), (prowA[48:56, :], crow), (prowA[56:64, :], norm1_g),
                           (prowA[64:72, :], norm2_g), (prowA[72:80, :], final_g)):
            items.append(((lambda e, d=dst, s=src: e.dma_start(out=d, in_=s)), [], ["prowA"]))
        S.dma_group("sp", items, "prowA")
        items = []
        for (dst, src) in ((prowB[0:16, :], conv_w), (prowB[16:20, :], conv_b), (prowB[20:28, :], lru_ba),
                           (prowB[28:36, :], lru_bx), (prowB[36:44, :], lru_lam), (prowB[44:48, :], attn_out_g),
                           (prowB[48:52, :], lru_out_g)):
            items.append(((lambda e, d=dst, s=src: e.dma_start(out=d, in_=s)), ["prowB"], ["prowB"]))
        S.dma_group("sp", items, "prowB")
        MM(pb[5][:, 0:80], [(prowA, ident_f[0:80, 0:80])], ["prowA", "ident_f"], ["pb0"])
        ACT(pcolA[:], pb[5][:, 0:80], AF.Copy, ["pb0"], ["pcolA"])
        MM(pb[6][:, 0:64], [(prowB, ident_f[0:64, 0:64])], ["prowB", "ident_f"], ["pb1"])
        ACT(pcolB[:], pb[6][:, 0:64], AF.Copy, ["pb1"], ["pcolB"])
        ACT(silu_c[:], pcolA[:, 48:56], AF.Silu, ["pcolA"], ["silu_c"])

        wab = [r_.rearrange("p (k n) -> p k n", k=8) for r_ in
               (R3[:, 0:4096], R3[:, 4096:8192], R1[:, 0:4096], R1[:, 4096:8192], R1[:, 8192:12288], R1[:, 12288:16384])]
        xts = [R4[:, i * 1024:(i + 1) * 1024] for i in range(3)]
        junk = bf(R4[:, 3072:3584])
        xss = [bf(R2[:, t * 512:(t + 1) * 512]) for t in range(NT)]

        def rms_tile(src_f32, ssq_col, ms_col, rstd_col, tok, extra_reads):
            ACT(junk, src_f32, AF.Square, extra_reads, ["junk", ("ssq", tok)], accum=ssq_col)
            TS("dve", ms_col, ssq_col, 1.0 / DM, EPS, ALU.mult, ALU.add, [("ssq", tok)], [("ms", tok)])
            TT("pool", rstd_col, ms_col, NEGHALF, ALU.pow, [("ms", tok), "cstf"], [("rstd", tok)])

        def stage1a(t):
            s3 = t % 3
            DMA("sp", xts[s3], x[t * 128:(t + 1) * 128, :], ("xt", s3), [], [("xt", s3)])
            rms_tile(xts[s3], ssq1[:, t:t + 1], ms1[:, t:t + 1], rstd1[:, t:t + 1], ("n1", t), [("xt", s3)])
            TS("dve", xss[t], xts[s3], rstd1[:, t:t + 1], None, ALU.mult, None, [("rstd", ("n1", t)), ("xt", s3)], [("xs", t)])

        def ada_buf(blk):
            return (2 + blk) if blk < 4 else (blk % 2)

        def ada_dma(blk):
            b_ = ada_buf(blk)
            DMA("sp", wab[b_], w_ada_v[:, :, blk * 512:(blk + 1) * 512], ("wab", b_), [], [("wab", b_)])

        S.alias(["pbm1"], ["pb1"])

        def ada_mm(blk):
            b_ = ada_buf(blk)
            bank = pb[6] if blk < 4 else pb[7]

            def fn(e, b_=b_, blk=blk, bank=bank):
                ins = None
                for jj in range(4):
                    j = blk * 4 + jj
                    for k in range(8):
                        ins = e.matmul(bank[:, j:j + 1], lhsT=wab[b_][:, k, jj * 128:(jj + 1) * 128],
                                       rhs=silu_c[:, k:k + 1], start=(k == 0), stop=(k == 7))
                return ins
            S.op("pe", fn, [("wab", b_), "silu_c"], ["pbm1" if blk < 4 else "pb2"])

        hT = bf(R1[:]).rearrange("p (k t) -> p k t", k=8)
        SH1 = modc[:, 0:8]
        SH2 = modc[:, 24:32]

        def stage1b(g4):
            pvs = [bf(pb[b_][:]).rearrange("p (k t) -> p k t", k=2) for b_ in range(4)]
            for b_ in range(4):
                TRS([(pvs[b_][:, kk, ti * 128:(ti + 1) * 128], xss[g4 * 4 + ti][:, (2 * b_ + kk) * 128:(2 * b_ + kk + 1) * 128])
                     for kk in range(2) for ti in range(4)], [("xs", g4 * 4 + ti) for ti in range(4)] + ["ident_bf"], [("pb", b_)])
            for k in range(8):
                b_, kk = k // 2, k % 2
                dst = hT[:, k, g4 * 512:(g4 + 1) * 512]
                if b_ % 2:
                    ACT(dst, pvs[b_][:, kk, :], AF.Identity, [("pb", b_), "gs1", "modcA"], [("hTa", g4)],
                        bias=SH1[:, k:k + 1], scale=gs1[:, k:k + 1])
                else:
                    TS("dve", dst, pvs[b_][:, kk, :], gs1[:, k:k + 1], SH1[:, k:k + 1], ALU.mult, ALU.add,
                       [("pb", b_), "gs1", "modcA"], [("hTd", g4)])

        tnext = 0
        gnext = 0
        for blk in range(6):
            ada_dma(blk)
        for blk in range(12):
            ada_mm(blk)
            if blk == 3:
                TT("dve", modc[:, 0:16], pb[6][:, 0:16], pcolA[:, 0:16], ALU.add, ["pbm1", "pcolA"], ["modcA"])
                STT(gs1[:], modc[:, 8:16], 1.0, pcolA[:, 56:64], ALU.add, ALU.mult, ["modcA", "pcolA"], ["gs1"])
                S.alias([("hTa", g) for g in range(8)] + [("hTd", g) for g in range(8)], [("wab", i) for i in range(2, 6)])
            for _ in range(3):
                if tnext < NT:
                    stage1a(tnext)
                    tnext += 1
            if 4 <= blk and blk + 2 < 12:
                ada_dma(blk + 2)
            if blk >= 4 and gnext < NT // 4 and tnext >= 4 * (gnext + 1):
                stage1b(gnext)
                gnext += 1
        while tnext < NT:
            stage1a(tnext)
            tnext += 1
        while gnext < NT // 4:
            stage1b(gnext)
            gnext += 1
        TT("dve", modc[:, 16:48], pb[7][:, 16:48], pcolA[:, 16:48], ALU.add, ["pb2", "pcolA"], ["modc"])
        STT(gs2[:], modc[:, 32:40], 1.0, pcolA[:, 64:72], ALU.add, ALU.mult, ["modc", "pcolA"], ["gs2"])

        ACT(cst1[:], pcolB[:, 36:44], AF.Exp, ["pcolB"], ["cst1"], scale=-1.0)
        TS("dve", cst2[:], cst1[:], -0.25, 1.0 / 3.0, ALU.mult, ALU.add, ["cst1"], ["cst2"])
        TT("dve", cst2[:], cst2[:], cst1[:], ALU.mult, ["cst2", "cst1"], ["cst2"])
        TS("dve", cst2[:], cst2[:], -1.0, 0.5, ALU.mult, ALU.add, ["cst2"], ["cst2"])
        TT("dve", cst2[:], cst2[:], cst1[:], ALU.mult, ["cst2", "cst1"], ["cst2"])
        TS("dve", cst2[:], cst2[:], -1.0, 1.0, ALU.mult, ALU.add, ["cst2"], ["cst2"])
        TT("dve", cst2[:], cst2[:], cst1[:], ALU.mult, ["cst2", "cst1"], ["cst2"])
        TS("dve", cs[:], cst2[:], -8.0, None, ALU.mult, None, ["cst2"], ["cs"])

        def bcast_rows(dst, col_ap, col_tok, dst_tok, scratch):
            for k in range(8):
                TS("dve", scratch[:, k * 128:(k + 1) * 128], ident_f[:], col_ap[:, k:k + 1], None, ALU.mult, None,
                   ["ident_f", col_tok], [("bcs", k)])
                MM(pb[3 + (k // 4)][:, (k % 4) * 128:(k % 4 + 1) * 128], [(ones_f[:], scratch[:, k * 128:(k + 1) * 128])],
                   [("bcs", k), "ones_f"], [("pbb", k)])
            for hh in range(2):
                ACT(dst[:, hh * 512:(hh + 1) * 512], pb[3 + hh][:, :], AF.Copy, [("pbb", 4 * hh + i) for i in range(4)], [dst_tok])


        if dbg == "hT":
            return finish([(bf(R1[:]), [("hTa", t) for t in range(8)] + [("hTd", t) for t in range(8)]), (modc[:], ["modc"]), (pcolB[:], ["pcolB"]), (cs[:], ["cs"])])

        hT_blk = lambda n: [("hTa", n), ("hTd", n)]

        S.barrier()
        HL = 2048
        T1 = R2[:, 0:4096]
        T2 = R2[:, 4096:8192]
        T8 = R2[:, 8192:12288]
        TA = [R2[:, 12288:14336], R2[:, 14336:16384]]
        T3 = bf(R3[:, 0:2048])
        T4 = bf(R3[:, 2048:4096])
        TB = [R3[:, 4096:6144], R3[:, 6144:8192]]
        wl = [bf(R4[:, i * 1024:(i + 1) * 1024]).rearrange("p (k n) -> p k n", k=8) for i in range(2)]
        recb = [bf(R4[:, 2048 + i * 1024:2048 + (i + 1) * 1024]) for i in range(2)]
        sqb = bf(R4[:, 4096:5120])
        TC = R4[:, 5120:7168]
        T9 = R5[:, 1024:3072]
        carry = colt[:, 0:1]
        halfb = sb("halfb", [128, 16], F32)
        hcs = sb("hcs", [128, 8], F32)
        S.op("pool", lambda e: e.memset(wablk, 0.0), [], ["wablk"])
        items = []
        for d_ in range(2):
            for gi, wsrc in enumerate((lru_wa, lru_wx)):
                for n_ in range(8):
                    idx = (d_ * 2 + gi) * 4 + n_ // 2
                    o_ = (n_ % 2) * 64
                    items.append(((lambda e, dd=wablk[o_:o_ + 64, idx, o_:o_ + 64], ss=wsrc[d_ * 8 + n_]: e.dma_start(out=dd, in_=ss)),
                                  ["wablk"], ["wablk"]))
        S.dma_group("pool", items, "wablk")
        TS("dve", halfb[:], pcolB[:, 20:36], 0.5, None, ALU.mult, None, ["pcolB"], ["halfb"])
        TS("dve", hcs[:], cs[:], 0.5, None, ALU.mult, None, ["cs"], ["hcs"])

        pbrot = [0]

        def next_pb(lo=0, n=4):
            i = lo + pbrot[0] % n
            pbrot[0] += 1
            return i

        def load_wl(c):
            cb = c % 2
            S.dma_group("pool", [
                ((lambda e, d=wl[cb][:, :, 0:128], s_=w_in_v[:, :, 1536 + c * 128:1536 + (c + 1) * 128]: e.dma_start(out=d, in_=s_)), [], [("wl", cb)]),
                ((lambda e, d=wl[cb][:, :, 128:256], s_=w_in_v[:, :, 2048 + c * 128:2048 + (c + 1) * 128]: e.dma_start(out=d, in_=s_)), [], [("wl", cb)]),
            ], ("wl", cb))

        def proj_xr(c):
            cb = c % 2
            for n in range(8):
                pi = next_pb()
                MM(pb[pi][:], [(wl[cb][:, k, 0:128], hT[:, k, n * 512:(n + 1) * 512]) for k in range(8)],
                   [("wl", cb)] + hT_blk(n), [("pb", pi)])
                ACT(T1[:, n * 512:(n + 1) * 512], pb[pi][:], AF.Copy, [("pb", pi)], ["T1"])

        def proj_gr(c):
            cb = c % 2
            for n in range(8):
                pi = next_pb()
                MM(pb[pi][:], [(wl[cb][:, k, 128:256], hT[:, k, n * 512:(n + 1) * 512]) for k in range(8)],
                   [("wl", cb)] + hT_blk(n), [("pb", pi)])
                ACT(T4[:, n * 512:(n + 1) * 512], pb[pi][:], AF.Gelu_apprx_tanh, [("pb", pi)], ["T4"])

        def conv(c):
            ACT(T2, T1, AF.Identity, ["T1", "pcolB"], ["T2"], bias=pcolB[:, 16 + c:17 + c], scale=pcolB[:, 8 + c:9 + c])
            STT(T2[:, 2:SEQ], T1[:, 0:SEQ - 2], pcolB[:, c:c + 1], T2[:, 2:SEQ], ALU.mult, ALU.add, ["T1", "T2", "pcolB"], ["T2"])
            STT(T2[:, 1:SEQ], T1[:, 0:SEQ - 1], pcolB[:, 4 + c:5 + c], T2[:, 1:SEQ], ALU.mult, ALU.add, ["T1", "T2", "pcolB"], ["T2"])
            STT(T2[:, 0:SEQ - 1], T1[:, 1:SEQ], pcolB[:, 12 + c:13 + c], T2[:, 0:SEQ - 1], ALU.mult, ALU.add, ["T1", "T2", "pcolB"], ["T2"])
            CP("dve", T3, T2, ["T2"], ["T3"])

        piece_ctr = [0]

        def gates_scan(c, d_, hh):
            lo = hh * HL
            bi_ = piece_ctr[0] % 2
            piece_ctr[0] += 1
            A, Bf = TA[bi_], TB[bi_]
            ta, tb = ("TA", bi_), ("TB", bi_)
            for gi, (Tg, tg) in enumerate(((A, ta), (Bf, tb))):
                bcol = d_ * 4 + c + (0 if gi == 0 else 8)
                for nb in range(HL // 512):
                    pi = next_pb()
                    MM(pb[pi][:], [(wablk[:, (d_ * 2 + gi) * 4 + c, :], T3[:, lo + nb * 512:lo + (nb + 1) * 512])],
                       ["wablk", "T3"], [("pb", pi)])
                    ACT(Tg[:, nb * 512:(nb + 1) * 512], pb[pi][:], AF.Tanh, [("pb", pi), "halfb"], [tg],
                        bias=halfb[:, bcol:bcol + 1], scale=0.5)
            ci = d_ * 4 + c
            ACT(TC, A, AF.Exp, [ta, "cs"], ["TC"], bias=cs[:, ci:ci + 1], scale=cs[:, ci:ci + 1])
            ACT(A, A, AF.Exp, [ta, "hcs"], [ta], bias=hcs[:, ci:ci + 1], scale=hcs[:, ci:ci + 1])
            ACT(TC, TC, AF.Sqrt, ["TC"], ["TC"], bias=0.25, scale=-0.25)
            STT(Bf, Bf, 1.0, T2[:, lo:lo + HL], ALU.add, ALU.mult, [tb, "T2"], [tb])
            TT("dve", Bf, Bf, TC, ALU.mult, [tb, "TC"], [tb])
            if d_ == 0:
                init = 0.0 if hh == 0 else T8[:, HL - 1:HL]
                S.op("dve", lambda e: e.tensor_tensor_scan(out=T8[:, lo:lo + HL], data0=A, data1=Bf, initial=init,
                                                            op0=ALU.mult, op1=ALU.add), [ta, tb, "T8"], ["T8"])
            else:
                init = 0.0 if hh == 1 else carry
                S.op("dve", lambda e: e.tensor_tensor_scan(out=T9[:, ::-1], data0=A[:, ::-1], data1=Bf[:, ::-1], initial=init,
                                                            op0=ALU.mult, op1=ALU.add), [ta, tb, "carry", "T9"], ["T9"])
                if hh == 1:
                    CP("dve", carry, T9[:, 0:1], ["T9"], ["carry"])

        def combine(c, hh):
            lo = hh * HL
            rb = recb[hh]
            TT("dve", TC, T8[:, lo:lo + HL], T9, ALU.add, ["T8", "T9", "TC"], ["TC"])
            TT("dve", rb, TC, T4[:, lo:lo + HL], ALU.mult, ["TC", "T4"], [("recb", hh)])
            ACT(sqb, rb, AF.Square, [("recb", hh)], ["sqb"])

            def fn(e):
                ins = None
                for tt in range(HL // 128):
                    ins = e.matmul(pb[4][:, hh * 16 + tt:hh * 16 + tt + 1], lhsT=sqb[:, tt * 128:(tt + 1) * 128],
                                   rhs=ones_bf[:, 0:1], start=True, stop=True)
                return ins
            S.op("pe", fn, ["sqb", "ones_bf"], [("pb", 4)])
            if c == 0:
                CP("dve", ssq_l[:, hh * 16:(hh + 1) * 16], pb[4][:, hh * 16:(hh + 1) * 16], [("pb", 4)], ["ssq_l"])
            else:
                TT("dve", ssq_l[:, hh * 16:(hh + 1) * 16], pb[4][:, hh * 16:(hh + 1) * 16], ssq_l[:, hh * 16:(hh + 1) * 16],
                   ALU.add, [("pb", 4), "ssq_l"], ["ssq_l"])
            DMA("sp", mixs[4 + c, :, lo:lo + HL], rb, ("recst", hh), [("recb", hh)], [])

        load_wl(0)
        proj_xr(0)
        for c in range(4):
            if c + 1 < 4:
                load_wl(c + 1)
            conv(c)
            proj_gr(c)
            gates_scan(c, 0, 0)
            if c + 1 < 4:
                proj_xr(c + 1)
            gates_scan(c, 0, 1)
            gates_scan(c, 1, 1)
            combine(c, 1)
            gates_scan(c, 1, 0)
            combine(c, 0)
        if dbg == "lru":
            return finish([(ssq_l[:], ["ssq_l"])])

        S.barrier()
        qT = bf(R2[:, 0:8192]).rearrange("p (k t) -> p k t", k=4)
        kT = bf(R2[:, 8192:16384]).rearrange("p (k t) -> p k t", k=4)
        vT = bf(R3[:]).rearrange("p (k t) -> p k t", k=4)
        wq = bf(R4[:, 0:4096]).rearrange("p (k n) -> p k n", k=8)
        posi = R4[:, 4096:4608].bitcast(I32)
        ang = R4[:, 4608:5120]
        kint = R4[:, 5120:5632].bitcast(I32)
        kfl = R4[:, 5632:6144]
        Ctabs = [R4[:, 6144:6656], R3[:, 0:512]]
        Stabs = [R4[:, 6656:7168], R3[:, 512:1024]]
        th = R5[:, 2560:3072]
        qbt = [bf(R5[:, 2048 + i * 256:2048 + (i + 1) * 256]) for i in range(2)]
        t1s = [R5[:, i * 512:(i + 1) * 512] for i in range(2)]
        t2s = [R5[:, 1024 + i * 512:1024 + (i + 1) * 512] for i in range(2)]

        S.dma_group("pool", [((lambda e, d=wq[:, k, :], s_=w_in_v[:, k, 0:1024]: e.dma_start(out=d, in_=s_)), [], ["wq"]) for k in range(8)], "wq")
        def rot_rest(n, j, pi, sl, blk):
            pj = 4 + pi
            Ctab, Stab = Ctabs[n % 2], Stabs[n % 2]
            MM(pb[pj][:], [(psw[:], qbt[sl])], ["psw", ("qbt", sl)], [("pb", pj)])
            TT("dve", t1s[sl], pb[pi][:], Ctab, ALU.mult, [("pb", pi), ("Ctab", n % 2), ("qbt", sl)], [("t1", sl)])
            TT("dve", t2s[sl], pb[pj][:], Stab, ALU.mult, [("pb", pj), ("Stab", n % 2)], [("t2", sl)])
            dst = (qT if j < 4 else kT)[:, j % 4, blk]
            TT("pool", dst, t1s[sl], t2s[sl], ALU.add, [("t1", sl), ("t2", sl)], [("qk", j)])

        def build_tables(n):
            blk = slice(n * 512, (n + 1) * 512)
            Ctab, Stab = Ctabs[n % 2], Stabs[n % 2]
            DMA("sp", posi, pos[0:1, blk].partition_broadcast(128), "posi", [], ["posi"])
            CP("dve", ang, posi, ["posi"], ["ang"])
            TS("dve", ang, ang, INVF, None, ALU.mult, None, ["ang", "cstf"], ["ang"])
            for which in range(2):
                if which == 1:
                    TS("dve", ang, ang, PI / 2, None, ALU.add, None, ["ang"], ["ang"])
                TS("dve", kint, ang, 1.0 / TWO_PI, None, ALU.mult, None, ["ang"], ["kint"])
                CP("dve", kfl, kint, ["kint"], ["kfl"])
                STT(th, kfl, -TWO_PI, ang, ALU.mult, ALU.add, ["kfl", "ang"], ["th"])
                TS("dve", th, th, -PI_C, PI_C, ALU.max, ALU.min, ["th"], ["th"])
                if which == 0:
                    ACT(Stab, th, AF.Sin, ["th", "cstf"], [("Stab", n % 2)], scale=SGN)
                else:
                    ACT(Ctab, th, AF.Sin, ["th"], [("Ctab", n % 2)])

        pend = None
        build_tables(0)
        for n in range(8):
            blk = slice(n * 512, (n + 1) * 512)
            for j in range(8):
                pi = next_pb()
                sl = (n * 8 + j) % 2
                MM(pb[pi][:], [(wq[:, k, j * 128:(j + 1) * 128], hT[:, k, blk]) for k in range(8)], ["wq"] + hT_blk(n), [("pb", pi)])
                ACT(qbt[sl], pb[pi][:], AF.Copy, [("pb", pi)], [("qbt", sl)])
                if pend is not None:
                    rot_rest(*pend)
                pend = (n, j, pi, sl, blk)
                if j == 3 and n + 1 < 8:
                    build_tables(n + 1)
            rot_rest(*pend)
            pend = None
        S.alias([("vT", 0)], [("Ctab", 1), ("Stab", 1)])
        wv = bf(R5[:, 0:2048]).rearrange("p (k n) -> p k n", k=8)
        S.dma_group("pool", [((lambda e, d=wv[:, k, :], s_=w_in_v[:, k, 1024:1536]: e.dma_start(out=d, in_=s_)), [], ["wv", ("t1", 0), ("t1", 1), ("t2", 0), ("t2", 1)]) for k in range(8)], "wv")
        for j in range(4):
            for n in range(8):
                pi = next_pb()
                MM(pb[pi][:], [(wv[:, k, j * 128:(j + 1) * 128], hT[:, k, n * 512:(n + 1) * 512]) for k in range(8)],
                   ["wv"] + hT_blk(n), [("pb", pi)])
                ACT(vT[:, j, n * 512:(n + 1) * 512], pb[pi][:], AF.Copy, [("pb", pi)], [("vT", j)])
        if dbg == "qkv":
            return finish([(bf(R2[:]), [("qk", j) for j in range(8)]), (bf(R3[:]), [("vT", j) for j in range(4)])])

        S.barrier()
        accS = [R1[:, 6400:10496], R1[:, 10496:14592]]
        aob = bf(R1[:, 0:2048])
        sqa = bf(R1[:, 2048:4096])
        NPT = 3
        pTs = [bf(R1[:, 4096 + i * 256:4096 + (i + 1) * 256]).rearrange("p (h n) -> p h n", h=2) for i in range(NPT)]
        vts = [bf(R1[:, 4864 + i * 128:4864 + (i + 1) * 128]).rearrange("p (h c) -> p h c", h=2) for i in range(NPT)]
        rls = [R1[:, 5248 + i * 512:5248 + (i + 1) * 512] for i in range(2)]
        for i in range(NPT):
            S.op("pool", lambda e, v_=vts[i]: e.memset(v_, 1.0), [], [("vt", i)])
        vtp = [bf(pb[2][:])[:, 0:128], bf(pb[3][:])[:, 0:128]]
        sTv = [pb[i][:].rearrange("p (h n) -> p h n", h=2) for i in range(2)]
        qz = [[bf(R4[:, 0:2048]), bf(R4[:, 2048:4096])], [bf(R4[:, 4096:6144]), bf(R5[:, 0:2048])]]
        for par in range(2):
            S.op("pool", lambda e, t_=qz[par][0][64:128, :]: e.memset(t_, 0.0), [], [("qz", par, 0)])
            S.op("pool", lambda e, t_=qz[par][1][0:64, :]: e.memset(t_, 0.0), [], [("qz", par, 1)])
        w2 = bf(R2[:]).rearrange("p (k n) -> p k n", k=32)

        def load_w2_group(q8):
            fbs = range(q8 * 4, q8 * 4 + 4)
            S.dma_group("pool", [((lambda e, d=w2[:, q8 * 4:q8 * 4 + 4, :], s_=w_ff2_v[:, q8 * 4:q8 * 4 + 4, :]: e.dma_start(out=d, in_=s_)),
                                  [], [("w2", fb) for fb in fbs])], ("w2g", q8))

        gb = [0]
        for ch in range(4):
            par = ch % 2
            if ch >= 1:
                S.alias([("w2", fb) for fb in range((ch - 1) * 4, (ch - 1) * 4 + 4)], [("qk", ch - 1)])
                S.alias([("w2", fb) for fb in range(16 + (ch - 1) * 4, 16 + (ch - 1) * 4 + 4)], [("qk", 4 + ch - 1)])
                load_w2_group(ch - 1)
                load_w2_group(4 + ch - 1)
            CP("dve", qz[par][0][0:64, :], qT[0:64, ch, :], [("qk", ch)], [("qz", par, 0)])
            CP("dve", qz[par][1][64:128, :], qT[64:128, ch, :], [("qk", ch)], [("qz", par, 1)])
            blocks = []
            for p_, d_ in enumerate(DILS):
                n_ = SEQ // d_
                nkb = n_ // 128
                nbank = n_ // 512 if n_ >= 512 else 1
                bankw = min(512, n_)
                for r in range(d_):
                    for kb in range(nkb):
                        q0 = max(0, 128 * kb - 64)
                        q1 = min(n_, 128 * kb + 192)
                        blocks.append(dict(p=p_, d=d_, r=r, kb=kb, q0=q0, q1=q1, N=q1 - q0, moff=q0 - (128 * kb - 64),
                                           nkb=nkb, nbank=nbank, bankw=bankw, gi=gb[0],
                                           ktok=slice(r + d_ * 128 * kb, r + d_ * (128 * kb + 127) + 1, d_),
                                           qtok=slice(r + d_ * q0, r + d_ * (q1 - 1) + 1, d_)))
                        gb[0] += 1
            started = {}

            def stageA(B):
                gi = B["gi"]
                vs, v3, N = gi % 2, gi % NPT, B["N"]
                TRS([(vtp[vs], vT[:, ch, B["ktok"]])], [("vT", ch), "ident_bf"], [("vtp", vs)])
                CP("dve", vts[v3][:, :, 0:64], vtp[vs].rearrange("p (h c) -> p h c", h=2), [("vtp", vs)], [("vt", v3)])
                def fn(e, vs=vs, N=N, B=B, ch=ch, par=par):
                    e.matmul(sTv[vs][:, 0, 0:N], lhsT=kT[:, ch, B["ktok"]], rhs=qz[par][0][:, B["qtok"]], start=True, stop=False, skip_group_check=True)
                    e.matmul(sTv[vs][:, 1, 0:N], lhsT=kT[:, ch, B["ktok"]], rhs=qz[par][1][:, B["qtok"]], start=False, stop=False, skip_group_check=True)
                    e.matmul(sTv[vs][:, 0, 0:N], lhsT=ident_bf[:], rhs=maskb[:, B["moff"]:B["moff"] + N],
                             start=False, stop=False, skip_group_check=True)
                    return e.matmul(sTv[vs][:, 1, 0:N], lhsT=ident_bf[:], rhs=maskb[:, B["moff"]:B["moff"] + N],
                                    start=False, stop=True, skip_group_check=True)
                S.op("pe", fn, [("qk", 4 + ch), ("qz", par, 0), ("qz", par, 1), "maskb", "ident_bf"], [("sT", vs)])

            def stageB(B):
                gi = B["gi"]
                vs, v3, N = gi % 2, gi % NPT, B["N"]
                ACT(pTs[v3][:, :, 0:N], sTv[vs][:, :, 0:N], AF.Exp, [("sT", vs)], [("pT", v3)], scale=0.125)

            def stageC(B):
                gi = B["gi"]
                v3, N, q0, q1, bankw, d_, r = gi % NPT, B["N"], B["q0"], B["q1"], B["bankw"], B["d"], B["r"]
                for hh in range(2):
                    a = q0
                    while a < q1:
                        bk = a // bankw
                        b_end = min(q1, (bk + 1) * bankw)
                        pbi = 4 + hh * 2 + (bk % 2)
                        key = (B["p"], r, hh, bk)
                        first = key not in started
                        started[key] = True
                        MM(pb[pbi][0:65, a - bk * bankw:b_end - bk * bankw],
                           [(vts[v3][:, hh, 0:65], pTs[v3][:, hh, a - q0:b_end - q0])],
                           [("vt", v3), ("pT", v3)], [("accP", pbi)], first_start=first, skip=True)
                        a = b_end
                    for bk in range(B["nbank"]):
                        last_kb = min(B["nkb"] - 1, (bk * bankw + bankw - 1 + 64) // 128)
                        if last_kb == B["kb"]:
                            pbi = 4 + hh * 2 + (bk % 2)
                            tok = slice(r + d_ * bk * bankw, r + d_ * (bk * bankw + bankw - 1) + 1, d_)
                            dstA = accS[hh][0:65, tok]
                            if B["p"] == 0:
                                CP("dve", dstA, pb[pbi][0:65, 0:bankw], [("accP", pbi)], [("accS", hh)])
                            else:
                                TT("dve", dstA, pb[pbi][0:65, 0:bankw], dstA, ALU.add, [("accP", pbi), ("accS", hh)], [("accS", hh)])

            stageA(blocks[0])
            for i_, B in enumerate(blocks):
                if i_ + 1 < len(blocks):
                    stageA(blocks[i_ + 1])
                stageB(B)
                stageC(B)
            for hh in range(2):
                for n in range(8):
                    blk = slice(n * 512, (n + 1) * 512)
                    rs = (hh * 8 + n) % 2
                    MM(pb[2 + rs][0:64, :], [(SEL, accS[hh][0:65, blk])], ["cstf", ("accS", hh)], [("vtp", rs)])
                    ACT(rls[rs][0:64, :], pb[2 + rs][0:64, :], AF.Ln, [("vtp", rs)], [("rl", rs)])
                    ACT(rls[rs][0:64, :], rls[rs][0:64, :], AF.Exp, [("rl", rs)], [("rl", rs)], scale=-1.0)
                    TT("dve", aob[64 * hh:64 * hh + 64, blk], accS[hh][0:64, blk], rls[rs][0:64, :], ALU.mult,
                       [("accS", hh), ("rl", rs)], [("aob", hh)])
            ACT(sqa, aob, AF.Square, [("aob", 0), ("aob", 1)], ["sqa"])

            def fn(e):
                ins = None
                for tt in range(NT):
                    ins = e.matmul(pb[3][:, tt:tt + 1], lhsT=sqa[:, tt * 128:(tt + 1) * 128], rhs=ones_bf[:, 0:1], start=True, stop=True)
                return ins
            S.op("pe", fn, ["sqa", "ones_bf"], [("vtp", 1)])
            if ch == 0:
                CP("dve", ssq_a[:], pb[3][:, 0:NT], [("vtp", 1)], ["ssq_a"])
            else:
                TT("dve", ssq_a[:], pb[3][:, 0:NT], ssq_a[:], ALU.add, [("vtp", 1), "ssq_a"], ["ssq_a"])
            DMA("sp", mixs[ch, :, :], aob, "aost", [("aob", 0), ("aob", 1)], [])
        if dbg == "attn":
            return finish([(ssq_a[:], ["ssq_a"]), (ssq_l[:], ["ssq_l"])])

        S.barrier()
        w1 = bf(R1[:]).rearrange("p (k n) -> p k n", k=8)
        wo = bf(R3[:, 0:4096]).rearrange("p (k n) -> p k n", k=8)
        for q8 in range(8):
            fbs = range(q8 * 4, q8 * 4 + 4)
            S.dma_group("pool", [((lambda e, d=w1[:, :, q8 * 512:(q8 + 1) * 512], s_=w_ff1_v[:, :, q8 * 512:(q8 + 1) * 512]: e.dma_start(out=d, in_=s_)),
                                  [], [("w1", fb) for fb in fbs])], ("w1g", q8))
            if q8 in (3, 7):
                load_w2_group(q8)
        NW = 4
        wtmp = [R4[:, i * 1024:(i + 1) * 1024] for i in range(NW)]
        g1_bc = R4[:, 4096:5120]
        g2_bc = R4[:, 5120:6144]
        bscr2 = R4[:, 6144:7168]
        bcast_rows(g1_bc, modc[:, 16:24], "modc", "g1_bc", bscr2)
        bcast_rows(g2_bc, modc[:, 40:48], "modc", "g2_bc", bscr2)
        bcast_rows(fg_bc, pcolA[:, 72:80], "pcolA", "fg_bc", bscr2)
        for ssq_, rstd_, tok in ((ssq_a, rstd_a, "rstd_a"), (ssq_l, rstd_l, "rstd_l")):
            TS("dve", ms1[:], ssq_[:], 1.0 / 512.0, EPS, ALU.mult, ALU.add, ["ssq_a", "ssq_l", "ms1"], ["ms1"])
            TT("pool", rstd_[:], ms1[:], neghalf32[:], ALU.pow, ["ms1", "neghalf32"], [tok])
        for k in range(8):
            s_ = k % NW
            DMA("sp", wtmp[s_], w_out_v[:, k, :], ("wtmp", s_), [], [("wtmp", s_)])
            STT(wo[:, k, :], wtmp[s_], pcolB[:, 44 + k:45 + k], g1_bc, ALU.mult, ALU.mult, [("wtmp", s_), "pcolB", "g1_bc"], ["wo"])
        S.alias(["mg"], [("wtmp", 0)])
        S.alias([("pb", 4)], [("pbb", k) for k in range(4, 8)])
        S.alias([("acc", 1, 1)], [("pbb", k) for k in range(4)])
        S.alias([("x1g", 0, 0), ("x1g", 0, 1)], [("wtmp", 1), ("wtmp", 2)])
        S.alias([("x1g", 1, 0), ("x1g", 1, 1)], [("wtmp", 3), "g1_bc"])
        G = 256
        NG = SEQ // G
        mg = bf(R4[:, 0:1024]).rearrange("p (k t) -> p k t", k=8)
        x1g = [R4[:, 1024 + i * 2048:1024 + (i + 1) * 2048].rearrange("p (t f) -> p t f", t=2) for i in range(2)]
        h2gs = [bf(R3[:, 4096 + i * 1024:4096 + (i + 1) * 1024]).rearrange("p (k t) -> p k t", k=8) for i in range(2)]
        hidb = [bf(R3[:, 6144 + i * 128:6144 + (i + 1) * 128]) for i in range(4)]
        tmpf = [R3[:, 6656 + i * 512:6656 + (i + 1) * 512] for i in range(2)] + [R4[:, 6144 + i * 512:6144 + (i + 1) * 512] for i in range(2)]
        S.alias([("tmpf", 2), ("tmpf", 3)], [("bcs", k) for k in range(8)])
        xs2 = [bf(R3[:, 7680:8192]), bf(R5[:, 192:704])]
        ssq2 = R5[:, 0:32]
        ms2 = R5[:, 32:64]
        rstd2 = R5[:, 64:96]
        ssq3 = R5[:, 96:128]
        ms3 = R5[:, 128:160]
        rstd3 = R5[:, 160:192]
        junk2 = bf(R5[:, 704:1216])
        hid = [R5[:, 2240 + i * 256:2240 + (i + 1) * 256] for i in range(3)]

        def pro_dma(g):
            DMA("sp", mg, mixs_v[:, :, g * G:(g + 1) * G], "mg", [], ["mg"])
            DMA("sp", x1g[g % 2], x[g * G:(g + 1) * G, :].rearrange("(t p) f -> p t f", p=128), ("xg", g % 2), [],
                [("x1g", g % 2, 0), ("x1g", g % 2, 1)])

        def pro_woutA(g, tt, hf):
            tile_i = g * 2 + tt
            cs_ = slice(hf * 512, (hf + 1) * 512)
            ts_ = (tt * 2 + hf) % 2
            MM(pb[4][:], [(mg[:, k, tt * 128:(tt + 1) * 128], wo[:, k, cs_]) for k in range(4)], ["mg", "wo"], [("pb", 4)])
            TS("dve", tmpf[ts_], pb[4][:], rstd_a[:, tile_i:tile_i + 1], None, ALU.mult, None, [("pb", 4), "rstd_a"], [("tmpf", ts_)])

        def pro_woutL(g, tt, hf):
            xb_ = g % 2
            tile_i = g * 2 + tt
            cs_ = slice(hf * 512, (hf + 1) * 512)
            ts_ = (tt * 2 + hf) % 2
            MM(pb[4][:], [(mg[:, k, tt * 128:(tt + 1) * 128], wo[:, k, cs_]) for k in range(4, 8)], ["mg", "wo"], [("pb", 4)])
            STT(tmpf[ts_], pb[4][:], rstd_l[:, tile_i:tile_i + 1], tmpf[ts_], ALU.mult, ALU.add, [("pb", 4), "rstd_l", ("tmpf", ts_)], [("tmpf", ts_)])
            TT("pool", x1g[xb_][:, tt, cs_], tmpf[ts_], x1g[xb_][:, tt, cs_], ALU.add, [("tmpf", ts_), ("x1g", xb_, tt)], [("x1g", xb_, tt)])

        def pro_norm(g, tt):
            xb_ = g % 2
            tile_i = g * 2 + tt
            tok = ("n2", tile_i)
            ACT(junk2, x1g[xb_][:, tt, :], AF.Square, [("x1g", xb_, tt)], ["junk2", ("ssq", tok)], accum=ssq2[:, tile_i:tile_i + 1])
            TS("dve", ms2[:, tile_i:tile_i + 1], ssq2[:, tile_i:tile_i + 1], 1.0 / DM, EPS, ALU.mult, ALU.add, [("ssq", tok)], [("ms", tok)])
            TT("pool", rstd2[:, tile_i:tile_i + 1], ms2[:, tile_i:tile_i + 1], NEGHALF, ALU.pow, [("ms", tok), "cstf"], [("rstd", tok)])
            TS("dve", xs2[tt], x1g[xb_][:, tt, :], rstd2[:, tile_i:tile_i + 1], None, ALU.mult, None, [("rstd", tok), ("x1g", xb_, tt)], [("xs2", tt)])

        def pro_tr(g, tt):
            h2g = h2gs[g % 2]
            pv = bf(pb[4][:]).rearrange("p (k t) -> p k t", k=8)
            TRS([(pv[:, k, :], xs2[tt][:, k * 128:(k + 1) * 128]) for k in range(8)], [("xs2", tt), "ident_bf"], [("pb", 4)])
            for k in range(8):
                TS("dve", h2g[:, k, tt * 128:(tt + 1) * 128], pv[:, k, :], gs2[:, k:k + 1], SH2[:, k:k + 1], ALU.mult, ALU.add,
                   [("pb", 4), "gs2", "modc"], [("h2g", g % 2, tt)])

        def ff1(g, fb):
            h2g = h2gs[g % 2]
            hs = fb % 3
            MM(pb[5 + hs][:, 0:256], [(w1[:, k, fb * 128:(fb + 1) * 128], h2g[:, k, :]) for k in range(8)],
               [("w1", fb), ("h2g", g % 2, 0), ("h2g", g % 2, 1)], [("pb", 5 + hs)])
            ACT(hid[hs], pb[5 + hs][:, 0:256], AF.Relu, [("pb", 5 + hs)], [("hid", hs)])
            TT("pool", hidb[fb % 4], hid[hs], hid[hs], ALU.mult, [("hid", hs)], [("hidb", fb % 4)])

        def ff2(g, fb):
            for tt in range(2):
                for hf in range(2):
                    MM(pb[tt * 2 + hf][:], [(hidb[fb % 4][:, tt * 128:(tt + 1) * 128], w2[:, fb, hf * 512:(hf + 1) * 512])],
                       [("hidb", fb % 4), ("w2", fb)], [("acc", tt, hf)], first_start=(fb == 0), skip=True)

        def epilogue(g):
            xb_ = g % 2
            for tt in range(2):
                tile_i = g * 2 + tt
                for hf in range(2):
                    cs_ = slice(hf * 512, (hf + 1) * 512)
                    ts_ = tt * 2 + hf
                    TT("dve", tmpf[ts_], pb[tt * 2 + hf][:], g2_bc[:, cs_], ALU.mult, [("acc", tt, hf), "g2_bc"], [("tmpf", ts_)])
            for tt in range(2):
                for hf in range(2):
                    cs_ = slice(hf * 512, (hf + 1) * 512)
                    ts_ = tt * 2 + hf
                    TT("pool", x1g[xb_][:, tt, cs_], tmpf[ts_], x1g[xb_][:, tt, cs_], ALU.add,
                       [("tmpf", ts_), ("x1g", xb_, tt)], [("x1g", xb_, tt)])
            for tt in range(2):
                tile_i = g * 2 + tt
                tok = ("n3", tile_i)
                ACT(junk2, x1g[xb_][:, tt, :], AF.Square, [("x1g", xb_, tt)], ["junk2", ("ssq", tok)], accum=ssq3[:, tile_i:tile_i + 1])
                TS("dve", ms3[:, tile_i:tile_i + 1], ssq3[:, tile_i:tile_i + 1], 1.0 / DM, EPS, ALU.mult, ALU.add, [("ssq", tok)], [("ms", tok)])
                TT("pool", rstd3[:, tile_i:tile_i + 1], ms3[:, tile_i:tile_i + 1], NEGHALF, ALU.pow, [("ms", tok), "cstf"], [("rstd", tok)])
                STT(x1g[xb_][:, tt, :], x1g[xb_][:, tt, :], rstd3[:, tile_i:tile_i + 1], fg_bc[:], ALU.mult, ALU.mult,
                    [("x1g", xb_, tt), ("rstd", tok), "fg_bc"], [("x1g", xb_, tt)])
                DMA("sp", y[tile_i * 128:(tile_i + 1) * 128, :], x1g[xb_][:, tt, :], ("yst", xb_), [("x1g", xb_, tt)], [])

        sched = {0: [("dma",)], 2: [("woutA", 0, 0)], 3: [("woutL", 0, 0)], 4: [("woutA", 0, 1)], 5: [("woutL", 0, 1)], 7: [("norm", 0)],
                 8: [("woutA", 1, 0)], 9: [("woutL", 1, 0)], 10: [("woutA", 1, 1)], 11: [("woutL", 1, 1)],
                 13: [("norm", 1)], 17: [("tr", 0)], 23: [("tr", 1)]}

        def run_pro(g, item):
            if item[0] == "dma":
                pro_dma(g)
            elif item[0] == "woutA":
                pro_woutA(g, item[1], item[2])
            elif item[0] == "woutL":
                pro_woutL(g, item[1], item[2])
            elif item[0] == "norm":
                pro_norm(g, item[1])
            else:
                pro_tr(g, item[1])

        for fbk in sorted(sched):
            for item in sched[fbk]:
                run_pro(0, item)
        ff1(0, 0)
        ff1(0, 1)
        for g in range(NG):
            for fb in range(32):
                if fb + 2 < 32:
                    ff1(g, fb + 2)
                ff2(g, fb)
                if g + 1 < NG:
                    for item in sched.get(fb, ()):
                        run_pro(g + 1, item)
            if g + 1 < NG:
                ff1(g + 1, 0)
                ff1(g + 1, 1)
            epilogue(g)
        return finish()


def make_consts():
    c = np.zeros((128, 512), np.float32)
    kk = np.arange(128)[:, None]
    qq = np.arange(256)[None, :]
    c[:, 0:256] = np.where((qq >= kk) & (qq <= kk + 128), 0.0, -240000.0)
    psw = np.zeros((128, 128), np.float32)
    invf = np.zeros(128, np.float32)
    sgn = np.zeros(128, np.float32)
    for hb in (0, 64):
        for j in range(8):
            psw[hb + j + 8, hb + j] = 1.0
            psw[hb + j, hb + j + 8] = 1.0
            f = np.float32(500000.0) ** (-np.float32(j) / np.float32(8))
            invf[hb + j] = f
            invf[hb + j + 8] = f
            sgn[hb + j] = -1.0
            sgn[hb + j + 8] = 1.0
    c[:, 256:384] = psw
    c[64, 384:448] = 1.0
    c[:, 448] = invf
    c[:, 449] = sgn
    c[:, 450] = -0.5
    return c


_CACHE = {}


def make_in_maps(inputs):
    f = lambda a: np.ascontiguousarray(np.asarray(a))
    shared = {
        "w_ada": f(inputs["w_ada"][0]), "b_ada": f(inputs["b_ada"][0]).reshape(48, 128),
        "norm1_g": f(inputs["norm1_g"][0]).reshape(8, 128), "norm2_g": f(inputs["norm2_g"][0]).reshape(8, 128),
        "final_g": f(inputs["final_g"]).reshape(8, 128), "w_in": f(inputs["w_in"][0]),
        "conv_w": f(inputs["conv_w"][0]).reshape(16, 128), "conv_b": f(inputs["conv_b"][0]).reshape(4, 128),
        "lru_wa": f(inputs["lru_wa"][0]).reshape(16, 64, 64), "lru_wx": f(inputs["lru_wx"][0]).reshape(16, 64, 64),
        "lru_ba": f(inputs["lru_ba"][0]).reshape(8, 128), "lru_bx": f(inputs["lru_bx"][0]).reshape(8, 128),
        "lru_lam": f(inputs["lru_lam"][0]).reshape(8, 128), "attn_out_g": f(inputs["attn_out_g"][0]).reshape(4, 128),
        "lru_out_g": f(inputs["lru_out_g"][0]).reshape(4, 128), "w_out": f(inputs["w_out"][0]),
        "w_ff1": f(inputs["w_ff1"][0]), "w_ff2": f(inputs["w_ff2"][0]), "cst": make_consts(),
    }
    maps = []
    for b in range(8):
        m = dict(shared)
        m["x"] = f(inputs["x"][b])
        m["crow"] = f(inputs["c"][b]).reshape(8, 128)
        m["pos"] = f(inputs["positions"][b]).reshape(1, SEQ).astype(np.int32)
        maps.append(m)
    return maps


def kernel(**inputs):
    if "nc" not in _CACHE:
        _CACHE["nc"] = build_program()
    nc = _CACHE["nc"]
    in_maps = make_in_maps(inputs)
    res = run_bass_kernel_spmd(nc, in_maps, core_ids=list(range(8)))
    return np.stack([np.asarray(r["y"]) for r in res.results], axis=0).astype(np.float32)
```

```python
import math
from contextlib import ExitStack

import numpy as np
import concourse.bass as bass
import concourse.mybir as mybir
from concourse.bass_utils import run_bass_kernel_spmd

F32 = mybir.dt.float32
BF16 = mybir.dt.bfloat16
I32 = mybir.dt.int32
AF = mybir.ActivationFunctionType
ALU = mybir.AluOpType

SEQ = 4096
DM = 1024
NT = SEQ // 128
EPS = 1e-6
PI = math.pi
TWO_PI = 2.0 * math.pi
PI_C = 3.1415925
DILS = (1, 4, 16)


class Sched:
    ENGS = ("pe", "act", "dve", "pool", "sp")

    def __init__(self, nc, stack):
        self.nc = nc
        self.stack = stack
        self.ops = {e: [] for e in self.ENGS}
        self.clock_sem = {e: stack.enter_context(nc.semaphore("clk_" + e)) for e in ("pe", "act", "dve", "pool")}
        self.clock = {e: 0 for e in self.clock_sem}
        self.seen = {e: {} for e in self.ENGS}
        self.lastw = {}
        self.readers = {}
        self.dma_sems = {}

    def dma_sem(self, key):
        if key not in self.dma_sems:
            self.dma_sems[key] = [self.stack.enter_context(self.nc.semaphore("dma_%d" % len(self.dma_sems))), 0]
        return self.dma_sems[key]

    def _deps(self, eng, reads, writes, waiter=None):
        waiter = waiter or eng
        need = {}

        def add(ev):
            if ev is None:
                return
            if need.get(ev[0], 0) < ev[1]:
                need[ev[0]] = ev[1]
        for t in reads:
            add(self.lastw.get(t))
        for t in writes:
            w = self.lastw.get(t)
            if w is not None and (w[2] != eng or eng in ("act", "dve", "pool")):
                add(w)
            for r in self.readers.get(t, ()):
                if r[2] != eng:
                    add(r)
        waits = []
        for sem, val in need.items():
            if self.seen[waiter].get(sem, 0) < val:
                self.seen[waiter][sem] = val
                waits.append((sem, val))
        return waits

    def _commit(self, ev, reads, writes):
        for t in reads:
            self.readers.setdefault(t, []).append(ev)
        for t in writes:
            self.lastw[t] = ev
            self.readers[t] = []

    def op(self, eng, fn, reads=(), writes=()):
        waits = self._deps(eng, reads, writes)
        self.clock[eng] += 1
        sem = self.clock_sem[eng]
        ev = (sem, self.clock[eng], eng)
        self._commit(ev, reads, writes)
        self.ops[eng].append((waits, fn, sem, 1))
        return ev

    def dma(self, queue, fn, key, reads=(), writes=()):
        self.ndma = getattr(self, "ndma", 0) + 1
        waits = self._deps("dma#%d" % self.ndma, reads, writes, waiter=queue)
        s = self.dma_sem(key)
        s[1] += 16
        ev = (s[0], s[1], "dma")
        self._commit(ev, reads, writes)
        self.ops[queue].append((waits, fn, s[0], 16))
        return ev

    def dma_group(self, queue, items, key):
        ev = None
        toks = []
        for fn, reads, writes in items:
            ev = self.dma(queue, fn, key, reads, writes)
            toks += list(writes)
        for t in toks:
            self.lastw[t] = ev
        return ev

    def alias(self, new_tokens, old_tokens):
        evs = []
        for t in old_tokens:
            if self.lastw.get(t) is not None:
                evs.append(self.lastw[t])
            evs += list(self.readers.get(t, ()))
        for t in new_tokens:
            self.lastw[t] = None
            self.readers[t] = [(e[0], e[1], "alias") for e in evs]

    def wait_all(self, eng, evs):
        waits = []
        for ev in evs:
            if self.seen[eng].get(ev[0], 0) < ev[1]:
                self.seen[eng][ev[0]] = ev[1]
                waits.append((ev[0], ev[1]))
        if waits:
            self.ops[eng].append((waits, None, None, 0))

    def barrier(self):
        evs = [(self.clock_sem[e], self.clock[e]) for e in self.clock_sem if self.clock[e] > 0]
        evs += [(s[0], s[1]) for s in self.dma_sems.values() if s[1] > 0]
        for e in self.ENGS:
            self.wait_all(e, evs)

    def emit(self, block):
        def run(name):
            def body(e):
                for waits, fn, sem, inc in self.ops[name]:
                    for (s, v) in waits:
                        e.wait_ge(s, v)
                    if fn is not None:
                        fn(e).then_inc(sem, inc)
            return body
        block.tensor(run("pe"))
        block.scalar(run("act"))
        block.vector(run("dve"))
        block.gpsimd(run("pool"))
        block.sync(run("sp"))


def build_program(dbg=None):
    nc = bass.Bass("TRN2", target_bir_lowering=False)

    def din(name, shape, dt=F32):
        return nc.dram_tensor(name, list(shape), dt, kind="ExternalInput").ap()

    x = din("x", [SEQ, DM])
    crow = din("crow", [8, 128])
    pos = din("pos", [1, SEQ], I32)
    w_ada = din("w_ada", [DM, 6 * DM])
    b_ada = din("b_ada", [48, 128])
    norm1_g = din("norm1_g", [8, 128])
    norm2_g = din("norm2_g", [8, 128])
    final_g = din("final_g", [8, 128])
    w_in = din("w_in", [DM, 2560])
    conv_w = din("conv_w", [16, 128])
    conv_b = din("conv_b", [4, 128])
    lru_wa = din("lru_wa", [16, 64, 64])
    lru_wx = din("lru_wx", [16, 64, 64])
    lru_ba = din("lru_ba", [8, 128])
    lru_bx = din("lru_bx", [8, 128])
    lru_lam = din("lru_lam", [8, 128])
    attn_out_g = din("attn_out_g", [4, 128])
    lru_out_g = din("lru_out_g", [4, 128])
    w_out = din("w_out", [DM, DM])
    w_ff1 = din("w_ff1", [DM, 4 * DM])
    w_ff2 = din("w_ff2", [4 * DM, DM])
    cst = din("cst", [128, 512])
    y = nc.dram_tensor("y", [SEQ, DM], F32, kind="ExternalOutput").ap()
    mixs = nc.dram_tensor("mixs", [8, 128, SEQ], BF16, kind="ExternalOutput" if dbg else "Internal").ap()
    if dbg:
        dbgf = nc.dram_tensor("dbgf", [128, 16384], F32, kind="ExternalOutput").ap()
        dbgb = nc.dram_tensor("dbgb", [128, 65536], BF16, kind="ExternalOutput").ap()

    w_in_v = w_in.rearrange("(k p) n -> p k n", p=128)
    w_ada_v = w_ada.rearrange("(k p) n -> p k n", p=128)
    w_out_v = w_out.rearrange("(k p) n -> p k n", p=128)
    w_ff1_v = w_ff1.rearrange("(k p) n -> p k n", p=128)
    w_ff2_v = w_ff2.rearrange("(k p) n -> p k n", p=128)
    mixs_v = mixs.rearrange("k p t -> p k t")

    with ExitStack() as st:
        S = Sched(nc, st)

        def sb(name, shape, dt):
            return st.enter_context(nc.sbuf_tensor(name, list(shape), dt))

        ident_f = sb("ident_f", [128, 128], F32)
        ident_bf = sb("ident_bf", [128, 128], BF16)
        ones_f = sb("ones_f", [128, 128], F32)
        ones_bf = sb("ones_bf", [128, 128], BF16)
        cstf = sb("cstf", [128, 512], F32)
        maskb = sb("maskb", [128, 256], BF16)
        psw = sb("psw", [128, 128], BF16)
        pcolA = sb("pcolA", [128, 80], F32)
        pcolB = sb("pcolB", [128, 64], F32)
        silu_c = sb("silu_c", [128, 8], F32)
        modc = sb("modc", [128, 48], F32)
        gs1 = sb("gs1", [128, 8], F32)
        gs2 = sb("gs2", [128, 8], F32)
        cs = sb("cs", [128, 8], F32)
        cst1 = sb("cst1", [128, 8], F32)
        cst2 = sb("cst2", [128, 8], F32)
        ssq1 = sb("ssq1", [128, NT], F32)
        ms1 = sb("ms1", [128, NT], F32)
        rstd1 = sb("rstd1", [128, NT], F32)
        ssq_a = sb("ssq_a", [128, NT], F32)
        ssq_l = sb("ssq_l", [128, NT], F32)
        rstd_a = sb("rstd_a", [128, NT], F32)
        rstd_l = sb("rstd_l", [128, NT], F32)
        colt = sb("colt", [128, 8], F32)
        neghalf32 = sb("neghalf32", [128, NT], F32)

        R1 = sb("R1", [128, 16384], F32)
        R2 = sb("R2", [128, 16384], F32)
        R3 = sb("R3", [128, 8192], F32)
        R4 = sb("R4", [128, 7168], F32)
        prowA = R4[0:80, 3584:3712]
        prowB = R4[0:64, 3712:3840]
        R5 = sb("R5", [128, 3072], F32)
        fg_bc = R5[:, 1216:2240]
        wablk = R5[:, 0:1024].bitcast(BF16).rearrange("p (i n) -> p i n", i=16)

        pb = [st.enter_context(nc.psum_tensor("pb%d" % i, [128, 512], F32)) for i in range(8)]

        def bf(ap):
            return ap.bitcast(BF16)

        MASK = cstf[:, 0:256]
        SEL = cstf[0:65, 384:448]
        INVF = cstf[:, 448:449]
        SGN = cstf[:, 449:450]
        NEGHALF = cstf[:, 450:451]

        def ACT(out, in_, func, reads, writes, bias=None, scale=None, accum=None):
            kw = {}
            if bias is not None:
                kw["bias"] = bias
            if scale is not None:
                kw["scale"] = scale
            if accum is not None:
                kw["accum_out"] = accum
            return S.op("act", lambda e: e.activation(out=out, in_=in_, func=func, **kw), reads, writes)

        def TS(eng, out, in0, s1, s2, op0, op1, reads, writes):
            if op1 is None:
                return S.op(eng, lambda e: e.tensor_scalar(out=out, in0=in0, scalar1=s1, scalar2=None, op0=op0), reads, writes)
            return S.op(eng, lambda e: e.tensor_scalar(out=out, in0=in0, scalar1=s1, scalar2=s2, op0=op0, op1=op1), reads, writes)

        def STT(out, in0, scalar, in1, op0, op1, reads, writes):
            return S.op("dve", lambda e: e.scalar_tensor_tensor(out=out, in0=in0, scalar=scalar, in1=in1, op0=op0, op1=op1), reads, writes)

        def TT(eng, out, in0, in1, op, reads, writes):
            return S.op(eng, lambda e: e.tensor_tensor(out=out, in0=in0, in1=in1, op=op), reads, writes)

        def CP(eng, out, in_, reads, writes):
            return S.op(eng, lambda e: e.tensor_copy(out=out, in_=in_), reads, writes)

        def MM(out, pairs, reads, writes, first_start=True, skip=False):
            def fn(e):
                ins = None
                n = len(pairs)
                for i, (l, r) in enumerate(pairs):
                    ins = e.matmul(out, lhsT=l, rhs=r, start=(first_start and i == 0), stop=(i == n - 1),
                                   skip_group_check=skip)
                return ins
            return S.op("pe", fn, reads, writes)

        def TRS(items, reads, writes):
            def fn(e):
                ins = None
                for o, i in items:
                    ins = e.transpose(out=o, in_=i, identity=ident_bf[:])
                return ins
            return S.op("pe", fn, reads, writes)

        def DMA(queue, out, in_, key, reads, writes):
            return S.dma(queue, lambda e: e.dma_start(out=out, in_=in_), key, reads, writes)

        def finish(dumps=()):
            evs = []
            fo = bo = 0
            for ap, toks in dumps:
                n = ap.shape[-1]
                if ap.dtype == F32:
                    evs.append(DMA("sp", dbgf[:, fo:fo + n], ap, "dbg", toks, []))
                    fo += n
                else:
                    evs.append(DMA("sp", dbgb[:, bo:bo + n], ap, "dbg", toks, []))
                    bo += n
            S.barrier()
            with nc.Block() as block:
                S.emit(block)
            return nc

        S.op("pool", lambda e: e.memset(ident_f[:], 1.0), [], ["ident_f"])
        S.op("pool", lambda e: e.affine_select(out=ident_f[:], in_=ident_f[:], pattern=[[-1, 128]],
                                               compare_op=ALU.is_equal, fill=0.0, base=0, channel_multiplier=1),
             ["ident_f"], ["ident_f"])
        S.op("pool", lambda e: e.memset(ones_f[:], 1.0), [], ["ones_f"])
        S.op("pool", lambda e: e.memset(ones_bf[:], 1.0), [], ["ones_bf"])
        S.op("pool", lambda e: e.memset(prowB, 0.0), [], ["prowB"])
        S.op("pool", lambda e: e.memset(neghalf32[:], -0.5), [], ["neghalf32"])
        CP("pool", ident_bf[:], ident_f[:], ["ident_f"], ["ident_bf"])
        DMA("sp", cstf[:], cst, "cst", [], ["cstf"])
        CP("dve", maskb[:], cstf[:, 0:256], ["cstf"], ["maskb"])
        CP("dve", psw[:], cstf[:, 256:384], ["cstf"], ["psw"])

        items = []
        for (dst, src) in ((prowA[0:48, :], b_ada), (prowA[48:56, :], crow), (prowA[56:64, :], norm1_g),
                           (prowA[64:72, :], norm2_g), (prowA[72:80, :], final_g)):
            items.append(((lambda e, d=dst, s=src: e.dma_start(out=d, in_=s)), [], ["prowA"]))
        S.dma_group("sp", items, "prowA")
        items = []
        for (dst, src) in ((prowB[0:16, :], conv_w), (prowB[16:20, :], conv_b), (prowB[20:28, :], lru_ba),
                           (prowB[28:36, :], lru_bx), (prowB[36:44, :], lru_lam), (prowB[44:48, :], attn_out_g),
                           (prowB[48:52, :], lru_out_g)):
            items.append(((lambda e, d=dst, s=src: e.dma_start(out=d, in_=s)), ["prowB"], ["prowB"]))
        S.dma_group("sp", items, "prowB")
        MM(pb[5][:, 0:80], [(prowA, ident_f[0:80, 0:80])], ["prowA", "ident_f"], ["pb0"])
        ACT(pcolA[:], pb[5][:, 0:80], AF.Copy, ["pb0"], ["pcolA"])
        MM(pb[6][:, 0:64], [(prowB, ident_f[0:64, 0:64])], ["prowB", "ident_f"], ["pb1"])
        ACT(pcolB[:], pb[6][:, 0:64], AF.Copy, ["pb1"], ["pcolB"])
        ACT(silu_c[:], pcolA[:, 48:56], AF.Silu, ["pcolA"], ["silu_c"])

        wab = [r_.rearrange("p (k n) -> p k n", k=8) for r_ in
               (R3[:, 0:4096], R3[:, 4096:8192], R1[:, 0:4096], R1[:, 4096:8192], R1[:, 8192:12288], R1[:, 12288:16384])]
        xts = [R4[:, i * 1024:(i + 1) * 1024] for i in range(3)]
        junk = bf(R4[:, 3072:3584])
        xss = [bf(R2[:, t * 512:(t + 1) * 512]) for t in range(NT)]

        def rms_tile(src_f32, ssq_col, ms_col, rstd_col, tok, extra_reads):
            ACT(junk, src_f32, AF.Square, extra_reads, ["junk", ("ssq", tok)], accum=ssq_col)
            TS("dve", ms_col, ssq_col, 1.0 / DM, EPS, ALU.mult, ALU.add, [("ssq", tok)], [("ms", tok)])
            TT("pool", rstd_col, ms_col, NEGHALF, ALU.pow, [("ms", tok), "cstf"], [("rstd", tok)])

        def stage1a(t):
            s3 = t % 3
            DMA("sp", xts[s3], x[t * 128:(t + 1) * 128, :], ("xt", s3), [], [("xt", s3)])
            rms_tile(xts[s3], ssq1[:, t:t + 1], ms1[:, t:t + 1], rstd1[:, t:t + 1], ("n1", t), [("xt", s3)])
            TS("dve", xss[t], xts[s3], rstd1[:, t:t + 1], None, ALU.mult, None, [("rstd", ("n1", t)), ("xt", s3)], [("xs", t)])

        def ada_buf(blk):
            return (2 + blk) if blk < 4 else (blk % 2)

        def ada_dma(blk):
            b_ = ada_buf(blk)
            DMA("sp", wab[b_], w_ada_v[:, :, blk * 512:(blk + 1) * 512], ("wab", b_), [], [("wab", b_)])

        S.alias(["pbm1"], ["pb1"])

        def ada_mm(blk):
            b_ = ada_buf(blk)
            bank = pb[6] if blk < 4 else pb[7]

            def fn(e, b_=b_, blk=blk, bank=bank):
                ins = None
                for jj in range(4):
                    j = blk * 4 + jj
                    for k in range(8):
                        ins = e.matmul(bank[:, j:j + 1], lhsT=wab[b_][:, k, jj * 128:(jj + 1) * 128],
                                       rhs=silu_c[:, k:k + 1], start=(k == 0), stop=(k == 7))
                return ins
            S.op("pe", fn, [("wab", b_), "silu_c"], ["pbm1" if blk < 4 else "pb2"])

        hT = bf(R1[:]).rearrange("p (k t) -> p k t", k=8)
        SH1 = modc[:, 0:8]
        SH2 = modc[:, 24:32]

        def stage1b(g4):
            pvs = [bf(pb[b_][:]).rearrange("p (k t) -> p k t", k=2) for b_ in range(4)]
            for b_ in range(4):
                TRS([(pvs[b_][:, kk, ti * 128:(ti + 1) * 128], xss[g4 * 4 + ti][:, (2 * b_ + kk) * 128:(2 * b_ + kk + 1) * 128])
                     for kk in range(2) for ti in range(4)], [("xs", g4 * 4 + ti) for ti in range(4)] + ["ident_bf"], [("pb", b_)])
            for k in range(8):
                b_, kk = k // 2, k % 2
                dst = hT[:, k, g4 * 512:(g4 + 1) * 512]
                if b_ % 2:
                    ACT(dst, pvs[b_][:, kk, :], AF.Identity, [("pb", b_), "gs1", "modcA"], [("hTa", g4)],
                        bias=SH1[:, k:k + 1], scale=gs1[:, k:k + 1])
                else:
                    TS("dve", dst, pvs[b_][:, kk, :], gs1[:, k:k + 1], SH1[:, k:k + 1], ALU.mult, ALU.add,
                       [("pb", b_), "gs1", "modcA"], [("hTd", g4)])

        tnext = 0
        gnext = 0
        for blk in range(6):
            ada_dma(blk)
        for blk in range(12):
            ada_mm(blk)
            if blk == 3:
                TT("dve", modc[:, 0:16], pb[6][:, 0:16], pcolA[:, 0:16], ALU.add, ["pbm1", "pcolA"], ["modcA"])
                STT(gs1[:], modc[:, 8:16], 1.0, pcolA[:, 56:64], ALU.add, ALU.mult, ["modcA", "pcolA"], ["gs1"])
                S.alias([("hTa", g) for g in range(8)] + [("hTd", g) for g in range(8)], [("wab", i) for i in range(2, 6)])
            for _ in range(3):
                if tnext < NT:
                    stage1a(tnext)
                    tnext += 1
            if 4 <= blk and blk + 2 < 12:
                ada_dma(blk + 2)
            if blk >= 4 and gnext < NT // 4 and tnext >= 4 * (gnext + 1):
                stage1b(gnext)
                gnext += 1
        while tnext < NT:
            stage1a(tnext)
            tnext += 1
        while gnext < NT // 4:
            stage1b(gnext)
            gnext += 1
        TT("dve", modc[:, 16:48], pb[7][:, 16:48], pcolA[:, 16:48], ALU.add, ["pb2", "pcolA"], ["modc"])
        STT(gs2[:], modc[:, 32:40], 1.0, pcolA[:, 64:72], ALU.add, ALU.mult, ["modc", "pcolA"], ["gs2"])

        ACT(cst1[:], pcolB[:, 36:44], AF.Exp, ["pcolB"], ["cst1"], scale=-1.0)
        TS("dve", cst2[:], cst1[:], -0.25, 1.0 / 3.0, ALU.mult, ALU.add, ["cst1"], ["cst2"])
        TT("dve", cst2[:], cst2[:], cst1[:], ALU.mult, ["cst2", "cst1"], ["cst2"])
        TS("dve", cst2[:], cst2[:], -1.0, 0.5, ALU.mult, ALU.add, ["cst2"], ["cst2"])
        TT("dve", cst2[:], cst2[:], cst1[:], ALU.mult, ["cst2", "cst1"], ["cst2"])
        TS("dve", cst2[:], cst2[:], -1.0, 1.0, ALU.mult, ALU.add, ["cst2"], ["cst2"])
        TT("dve", cst2[:], cst2[:], cst1[:], ALU.mult, ["cst2", "cst1"], ["cst2"])
        TS("dve", cs[:], cst2[:], -8.0, None, ALU.mult, None, ["cst2"], ["cs"])

        def bcast_rows(dst, col_ap, col_tok, dst_tok, scratch):
            for k in range(8):
                TS("dve", scratch[:, k * 128:(k + 1) * 128], ident_f[:], col_ap[:, k:k + 1], None, ALU.mult, None,
                   ["ident_f", col_tok], [("bcs", k)])
                MM(pb[3 + (k // 4)][:, (k % 4) * 128:(k % 4 + 1) * 128], [(ones_f[:], scratch[:, k * 128:(k + 1) * 128])],
                   [("bcs", k), "ones_f"], [("pbb", k)])
            for hh in range(2):
                ACT(dst[:, hh * 512:(hh + 1) * 512], pb[3 + hh][:, :], AF.Copy, [("pbb", 4 * hh + i) for i in range(4)], [dst_tok])


        if dbg == "hT":
            return finish([(bf(R1[:]), [("hTa", t) for t in range(8)] + [("hTd", t) for t in range(8)]), (modc[:], ["modc"]), (pcolB[:], ["pcolB"]), (cs[:], ["cs"])])

        hT_blk = lambda n: [("hTa", n), ("hTd", n)]

        S.barrier()
        HL = 2048
        T1 = R2[:, 0:4096]
        T2 = R2[:, 4096:8192]
        T8 = R2[:, 8192:12288]
        TA = [R2[:, 12288:14336], R2[:, 14336:16384]]
        T3 = bf(R3[:, 0:2048])
        T4 = bf(R3[:, 2048:4096])
        TB = [R3[:, 4096:6144], R3[:, 6144:8192]]
        wl = [bf(R4[:, i * 1024:(i + 1) * 1024]).rearrange("p (k n) -> p k n", k=8) for i in range(2)]
        recb = [bf(R4[:, 2048 + i * 1024:2048 + (i + 1) * 1024]) for i in range(2)]
        sqb = bf(R4[:, 4096:5120])
        TC = R4[:, 5120:7168]
        T9 = R5[:, 1024:3072]
        carry = colt[:, 0:1]
        halfb = sb("halfb", [128, 16], F32)
        hcs = sb("hcs", [128, 8], F32)
        S.op("pool", lambda e: e.memset(wablk, 0.0), [], ["wablk"])
        items = []
        for d_ in range(2):
            for gi, wsrc in enumerate((lru_wa, lru_wx)):
                for n_ in range(8):
                    idx = (d_ * 2 + gi) * 4 + n_ // 2
                    o_ = (n_ % 2) * 64
                    items.append(((lambda e, dd=wablk[o_:o_ + 64, idx, o_:o_ + 64], ss=wsrc[d_ * 8 + n_]: e.dma_start(out=dd, in_=ss)),
                                  ["wablk"], ["wablk"]))
        S.dma_group("pool", items, "wablk")
        TS("dve", halfb[:], pcolB[:, 20:36], 0.5, None, ALU.mult, None, ["pcolB"], ["halfb"])
        TS("dve", hcs[:], cs[:], 0.5, None, ALU.mult, None, ["cs"], ["hcs"])

        pbrot = [0]

        def next_pb(lo=0, n=4):
            i = lo + pbrot[0] % n
            pbrot[0] += 1
            return i

        def load_wl(c):
            cb = c % 2
            S.dma_group("pool", [
                ((lambda e, d=wl[cb][:, :, 0:128], s_=w_in_v[:, :, 1536 + c * 128:1536 + (c + 1) * 128]: e.dma_start(out=d, in_=s_)), [], [("wl", cb)]),
                ((lambda e, d=wl[cb][:, :, 128:256], s_=w_in_v[:, :, 2048 + c * 128:2048 + (c + 1) * 128]: e.dma_start(out=d, in_=s_)), [], [("wl", cb)]),
            ], ("wl", cb))

        def proj_xr(c):
            cb = c % 2
            for n in range(8):
                pi = next_pb()
                MM(pb[pi][:], [(wl[cb][:, k, 0:128], hT[:, k, n * 512:(n + 1) * 512]) for k in range(8)],
                   [("wl", cb)] + hT_blk(n), [("pb", pi)])
                ACT(T1[:, n * 512:(n + 1) * 512], pb[pi][:], AF.Copy, [("pb", pi)], ["T1"])

        def proj_gr(c):
            cb = c % 2
            for n in range(8):
                pi = next_pb()
                MM(pb[pi][:], [(wl[cb][:, k, 128:256], hT[:, k, n * 512:(n + 1) * 512]) for k in range(8)],
                   [("wl", cb)] + hT_blk(n), [("pb", pi)])
                ACT(T4[:, n * 512:(n + 1) * 512], pb[pi][:], AF.Gelu_apprx_tanh, [("pb", pi)], ["T4"])

        def conv(c):
            ACT(T2, T1, AF.Identity, ["T1", "pcolB"], ["T2"], bias=pcolB[:, 16 + c:17 + c], scale=pcolB[:, 8 + c:9 + c])
            STT(T2[:, 2:SEQ], T1[:, 0:SEQ - 2], pcolB[:, c:c + 1], T2[:, 2:SEQ], ALU.mult, ALU.add, ["T1", "T2", "pcolB"], ["T2"])
            STT(T2[:, 1:SEQ], T1[:, 0:SEQ - 1], pcolB[:, 4 + c:5 + c], T2[:, 1:SEQ], ALU.mult, ALU.add, ["T1", "T2", "pcolB"], ["T2"])
            STT(T2[:, 0:SEQ - 1], T1[:, 1:SEQ], pcolB[:, 12 + c:13 + c], T2[:, 0:SEQ - 1], ALU.mult, ALU.add, ["T1", "T2", "pcolB"], ["T2"])
            CP("dve", T3, T2, ["T2"], ["T3"])

        piece_ctr = [0]

        def gates_scan(c, d_, hh):
            lo = hh * HL
            bi_ = piece_ctr[0] % 2
            piece_ctr[0] += 1
            A, Bf = TA[bi_], TB[bi_]
            ta, tb = ("TA", bi_), ("TB", bi_)
            for gi, (Tg, tg) in enumerate(((A, ta), (Bf, tb))):
                bcol = d_ * 4 + c + (0 if gi == 0 else 8)
                for nb in range(HL // 512):
                    pi = next_pb()
                    MM(pb[pi][:], [(wablk[:, (d_ * 2 + gi) * 4 + c, :], T3[:, lo + nb * 512:lo + (nb + 1) * 512])],
                       ["wablk", "T3"], [("pb", pi)])
                    ACT(Tg[:, nb * 512:(nb + 1) * 512], pb[pi][:], AF.Tanh, [("pb", pi), "halfb"], [tg],
                        bias=halfb[:, bcol:bcol + 1], scale=0.5)
            ci = d_ * 4 + c
            ACT(TC, A, AF.Exp, [ta, "cs"], ["TC"], bias=cs[:, ci:ci + 1], scale=cs[:, ci:ci + 1])
            ACT(A, A, AF.Exp, [ta, "hcs"], [ta], bias=hcs[:, ci:ci + 1], scale=hcs[:, ci:ci + 1])
            ACT(TC, TC, AF.Sqrt, ["TC"], ["TC"], bias=0.25, scale=-0.25)
            STT(Bf, Bf, 1.0, T2[:, lo:lo + HL], ALU.add, ALU.mult, [tb, "T2"], [tb])
            TT("dve", Bf, Bf, TC, ALU.mult, [tb, "TC"], [tb])
            if d_ == 0:
                init = 0.0 if hh == 0 else T8[:, HL - 1:HL]
                S.op("dve", lambda e: e.tensor_tensor_scan(out=T8[:, lo:lo + HL], data0=A, data1=Bf, initial=init,
                                                            op0=ALU.mult, op1=ALU.add), [ta, tb, "T8"], ["T8"])
            else:
                init = 0.0 if hh == 1 else carry
                S.op("dve", lambda e: e.tensor_tensor_scan(out=T9[:, ::-1], data0=A[:, ::-1], data1=Bf[:, ::-1], initial=init,
                                                            op0=ALU.mult, op1=ALU.add), [ta, tb, "carry", "T9"], ["T9"])
                if hh == 1:
                    CP("dve", carry, T9[:, 0:1], ["T9"], ["carry"])

        def combine(c, hh):
            lo = hh * HL
            rb = recb[hh]
            TT("dve", TC, T8[:, lo:lo + HL], T9, ALU.add, ["T8", "T9", "TC"], ["TC"])
            TT("dve", rb, TC, T4[:, lo:lo + HL], ALU.mult, ["TC", "T4"], [("recb", hh)])
            ACT(sqb, rb, AF.Square, [("recb", hh)], ["sqb"])

            def fn(e):
                ins = None
                for tt in range(HL // 128):
                    ins = e.matmul(pb[4][:, hh * 16 + tt:hh * 16 + tt + 1], lhsT=sqb[:, tt * 128:(tt + 1) * 128],
                                   rhs=ones_bf[:, 0:1], start=True, stop=True)
                return ins
            S.op("pe", fn, ["sqb", "ones_bf"], [("pb", 4)])
            if c == 0:
                CP("dve", ssq_l[:, hh * 16:(hh + 1) * 16], pb[4][:, hh * 16:(hh + 1) * 16], [("pb", 4)], ["ssq_l"])
            else:
                TT("dve", ssq_l[:, hh * 16:(hh + 1) * 16], pb[4][:, hh * 16:(hh + 1) * 16], ssq_l[:, hh * 16:(hh + 1) * 16],
                   ALU.add, [("pb", 4), "ssq_l"], ["ssq_l"])
            DMA("sp", mixs[4 + c, :, lo:lo + HL], rb, ("recst", hh), [("recb", hh)], [])

        load_wl(0)
        proj_xr(0)
        for c in range(4):
            if c + 1 < 4:
                load_wl(c + 1)
            conv(c)
            proj_gr(c)
            gates_scan(c, 0, 0)
            if c + 1 < 4:
                proj_xr(c + 1)
            gates_scan(c, 0, 1)
            gates_scan(c, 1, 1)
            combine(c, 1)
            gates_scan(c, 1, 0)
            combine(c, 0)
        if dbg == "lru":
            return finish([(ssq_l[:], ["ssq_l"])])

        S.barrier()
        qT = bf(R2[:, 0:8192]).rearrange("p (k t) -> p k t", k=4)
        kT = bf(R2[:, 8192:16384]).rearrange("p (k t) -> p k t", k=4)
        vT = bf(R3[:]).rearrange("p (k t) -> p k t", k=4)
        wq = bf(R4[:, 0:4096]).rearrange("p (k n) -> p k n", k=8)
        posi = R4[:, 4096:4608].bitcast(I32)
        ang = R4[:, 4608:5120]
        kint = R4[:, 5120:5632].bitcast(I32)
        kfl = R4[:, 5632:6144]
        Ctabs = [R4[:, 6144:6656], R3[:, 0:512]]
        Stabs = [R4[:, 6656:7168], R3[:, 512:1024]]
        th = R5[:, 2560:3072]
        qbt = [bf(R5[:, 2048 + i * 256:2048 + (i + 1) * 256]) for i in range(2)]
        t1s = [R5[:, i * 512:(i + 1) * 512] for i in range(2)]
        t2s = [R5[:, 1024 + i * 512:1024 + (i + 1) * 512] for i in range(2)]

        S.dma_group("pool", [((lambda e, d=wq[:, k, :], s_=w_in_v[:, k, 0:1024]: e.dma_start(out=d, in_=s_)), [], ["wq"]) for k in range(8)], "wq")
        def rot_rest(n, j, pi, sl, blk):
            pj = 4 + pi
            Ctab, Stab = Ctabs[n % 2], Stabs[n % 2]
            MM(pb[pj][:], [(psw[:], qbt[sl])], ["psw", ("qbt", sl)], [("pb", pj)])
            TT("dve", t1s[sl], pb[pi][:], Ctab, ALU.mult, [("pb", pi), ("Ctab", n % 2), ("qbt", sl)], [("t1", sl)])
            TT("dve", t2s[sl], pb[pj][:], Stab, ALU.mult, [("pb", pj), ("Stab", n % 2)], [("t2", sl)])
            dst = (qT if j < 4 else kT)[:, j % 4, blk]
            TT("pool", dst, t1s[sl], t2s[sl], ALU.add, [("t1", sl), ("t2", sl)], [("qk", j)])

        def build_tables(n):
            blk = slice(n * 512, (n + 1) * 512)
            Ctab, Stab = Ctabs[n % 2], Stabs[n % 2]
            DMA("sp", posi, pos[0:1, blk].partition_broadcast(128), "posi", [], ["posi"])
            CP("dve", ang, posi, ["posi"], ["ang"])
            TS("dve", ang, ang, INVF, None, ALU.mult, None, ["ang", "cstf"], ["ang"])
            for which in range(2):
                if which == 1:
                    TS("dve", ang, ang, PI / 2, None, ALU.add, None, ["ang"], ["ang"])
                TS("dve", kint, ang, 1.0 / TWO_PI, None, ALU.mult, None, ["ang"], ["kint"])
                CP("dve", kfl, kint, ["kint"], ["kfl"])
                STT(th, kfl, -TWO_PI, ang, ALU.mult, ALU.add, ["kfl", "ang"], ["th"])
                TS("dve", th, th, -PI_C, PI_C, ALU.max, ALU.min, ["th"], ["th"])
                if which == 0:
                    ACT(Stab, th, AF.Sin, ["th", "cstf"], [("Stab", n % 2)], scale=SGN)
                else:
                    ACT(Ctab, th, AF.Sin, ["th"], [("Ctab", n % 2)])

        pend = None
        build_tables(0)
        for n in range(8):
            blk = slice(n * 512, (n + 1) * 512)
            for j in range(8):
                pi = next_pb()
                sl = (n * 8 + j) % 2
                MM(pb[pi][:], [(wq[:, k, j * 128:(j + 1) * 128], hT[:, k, blk]) for k in range(8)], ["wq"] + hT_blk(n), [("pb", pi)])
                ACT(qbt[sl], pb[pi][:], AF.Copy, [("pb", pi)], [("qbt", sl)])
                if pend is not None:
                    rot_rest(*pend)
                pend = (n, j, pi, sl, blk)
                if j == 3 and n + 1 < 8:
                    build_tables(n + 1)
            rot_rest(*pend)
            pend = None
        S.alias([("vT", 0)], [("Ctab", 1), ("Stab", 1)])
        wv = bf(R5[:, 0:2048]).rearrange("p (k n) -> p k n", k=8)
        S.dma_group("pool", [((lambda e, d=wv[:, k, :], s_=w_in_v[:, k, 1024:1536]: e.dma_start(out=d, in_=s_)), [], ["wv", ("t1", 0), ("t1", 1), ("t2", 0), ("t2", 1)]) for k in range(8)], "wv")
        for j in range(4):
            for n in range(8):
                pi = next_pb()
                MM(pb[pi][:], [(wv[:, k, j * 128:(j + 1) * 128], hT[:, k, n * 512:(n + 1) * 512]) for k in range(8)],
                   ["wv"] + hT_blk(n), [("pb", pi)])
                ACT(vT[:, j, n * 512:(n + 1) * 512], pb[pi][:], AF.Copy, [("pb", pi)], [("vT", j)])
        if dbg == "qkv":
            return finish([(bf(R2[:]), [("qk", j) for j in range(8)]), (bf(R3[:]), [("vT", j) for j in range(4)])])

        S.barrier()
        accS = [R1[:, 6400:10496], R1[:, 10496:14592]]
        aob = bf(R1[:, 0:2048])
        sqa = bf(R1[:, 2048:4096])
        NPT = 3
        pTs = [bf(R1[:, 4096 + i * 256:4096 + (i + 1) * 256]).rearrange("p (h n) -> p h n", h=2) for i in range(NPT)]
        vts = [bf(R1[:, 4864 + i * 128:4864 + (i + 1) * 128]).rearrange("p (h c) -> p h c", h=2) for i in range(NPT)]
        rls = [R1[:, 5248 + i * 512:5248 + (i + 1) * 512] for i in range(2)]
        for i in range(NPT):
            S.op("pool", lambda e, v_=vts[i]: e.memset(v_, 1.0), [], [("vt", i)])
        vtp = [bf(pb[2][:])[:, 0:128], bf(pb[3][:])[:, 0:128]]
        sTv = [pb[i][:].rearrange("p (h n) -> p h n", h=2) for i in range(2)]
        qz = [[bf(R4[:, 0:2048]), bf(R4[:, 2048:4096])], [bf(R4[:, 4096:6144]), bf(R5[:, 0:2048])]]
        for par in range(2):
            S.op("pool", lambda e, t_=qz[par][0][64:128, :]: e.memset(t_, 0.0), [], [("qz", par, 0)])
            S.op("pool", lambda e, t_=qz[par][1][0:64, :]: e.memset(t_, 0.0), [], [("qz", par, 1)])
        w2 = bf(R2[:]).rearrange("p (k n) -> p k n", k=32)

        def load_w2_group(q8):
            fbs = range(q8 * 4, q8 * 4 + 4)
            S.dma_group("pool", [((lambda e, d=w2[:, q8 * 4:q8 * 4 + 4, :], s_=w_ff2_v[:, q8 * 4:q8 * 4 + 4, :]: e.dma_start(out=d, in_=s_)),
                                  [], [("w2", fb) for fb in fbs])], ("w2g", q8))

        gb = [0]
        for ch in range(4):
            par = ch % 2
            if ch >= 1:
                S.alias([("w2", fb) for fb in range((ch - 1) * 4, (ch - 1) * 4 + 4)], [("qk", ch - 1)])
                S.alias([("w2", fb) for fb in range(16 + (ch - 1) * 4, 16 + (ch - 1) * 4 + 4)], [("qk", 4 + ch - 1)])
                load_w2_group(ch - 1)
                load_w2_group(4 + ch - 1)
            CP("dve", qz[par][0][0:64, :], qT[0:64, ch, :], [("qk", ch)], [("qz", par, 0)])
            CP("dve", qz[par][1][64:128, :], qT[64:128, ch, :], [("qk", ch)], [("qz", par, 1)])
            blocks = []
            for p_, d_ in enumerate(DILS):
                n_ = SEQ // d_
                nkb = n_ // 128
                nbank = n_ // 512 if n_ >= 512 else 1
                bankw = min(512, n_)
                for r in range(d_):
                    for kb in range(nkb):
                        q0 = max(0, 128 * kb - 64)
                        q1 = min(n_, 128 * kb + 192)
                        blocks.append(dict(p=p_, d=d_, r=r, kb=kb, q0=q0, q1=q1, N=q1 - q0, moff=q0 - (128 * kb - 64),
                                           nkb=nkb, nbank=nbank, bankw=bankw, gi=gb[0],
                                           ktok=slice(r + d_ * 128 * kb, r + d_ * (128 * kb + 127) + 1, d_),
                                           qtok=slice(r + d_ * q0, r + d_ * (q1 - 1) + 1, d_)))
                        gb[0] += 1
            started = {}

            def stageA(B):
                gi = B["gi"]
                vs, v3, N = gi % 2, gi % NPT, B["N"]
                TRS([(vtp[vs], vT[:, ch, B["ktok"]])], [("vT", ch), "ident_bf"], [("vtp", vs)])
                CP("dve", vts[v3][:, :, 0:64], vtp[vs].rearrange("p (h c) -> p h c", h=2), [("vtp", vs)], [("vt", v3)])
                def fn(e, vs=vs, N=N, B=B, ch=ch, par=par):
                    e.matmul(sTv[vs][:, 0, 0:N], lhsT=kT[:, ch, B["ktok"]], rhs=qz[par][0][:, B["qtok"]], start=True, stop=False, skip_group_check=True)
                    e.matmul(sTv[vs][:, 1, 0:N], lhsT=kT[:, ch, B["ktok"]], rhs=qz[par][1][:, B["qtok"]], start=False, stop=False, skip_group_check=True)
                    e.matmul(sTv[vs][:, 0, 0:N], lhsT=ident_bf[:], rhs=maskb[:, B["moff"]:B["moff"] + N],
                             start=False, stop=False, skip_group_check=True)
                    return e.matmul(sTv[vs][:, 1, 0:N], lhsT=ident_bf[:], rhs=maskb[:, B["moff"]:B["moff"] + N],
                                    start=False, stop=True, skip_group_check=True)
                S.op("pe", fn, [("qk", 4 + ch), ("qz", par, 0), ("qz", par, 1), "maskb", "ident_bf"], [("sT", vs)])

            def stageB(B):
                gi = B["gi"]
                vs, v3, N = gi % 2, gi % NPT, B["N"]
                ACT(pTs[v3][:, :, 0:N], sTv[vs][:, :, 0:N], AF.Exp, [("sT", vs)], [("pT", v3)], scale=0.125)

            def stageC(B):
                gi = B["gi"]
                v3, N, q0, q1, bankw, d_, r = gi % NPT, B["N"], B["q0"], B["q1"], B["bankw"], B["d"], B["r"]
                for hh in range(2):
                    a = q0
                    while a < q1:
                        bk = a // bankw
                        b_end = min(q1, (bk + 1) * bankw)
                        pbi = 4 + hh * 2 + (bk % 2)
                        key = (B["p"], r, hh, bk)
                        first = key not in started
                        started[key] = True
                        MM(pb[pbi][0:65, a - bk * bankw:b_end - bk * bankw],
                           [(vts[v3][:, hh, 0:65], pTs[v3][:, hh, a - q0:b_end - q0])],
                           [("vt", v3), ("pT", v3)], [("accP", pbi)], first_start=first, skip=True)
                        a = b_end
                    for bk in range(B["nbank"]):
                        last_kb = min(B["nkb"] - 1, (bk * bankw + bankw - 1 + 64) // 128)
                        if last_kb == B["kb"]:
                            pbi = 4 + hh * 2 + (bk % 2)
                            tok = slice(r + d_ * bk * bankw, r + d_ * (bk * bankw + bankw - 1) + 1, d_)
                            dstA = accS[hh][0:65, tok]
                            if B["p"] == 0:
                                CP("dve", dstA, pb[pbi][0:65, 0:bankw], [("accP", pbi)], [("accS", hh)])
                            else:
                                TT("dve", dstA, pb[pbi][0:65, 0:bankw], dstA, ALU.add, [("accP", pbi), ("accS", hh)], [("accS", hh)])

            stageA(blocks[0])
            for i_, B in enumerate(blocks):
                if i_ + 1 < len(blocks):
                    stageA(blocks[i_ + 1])
                stageB(B)
                stageC(B)
            for hh in range(2):
                for n in range(8):
                    blk = slice(n * 512, (n + 1) * 512)
                    rs = (hh * 8 + n) % 2
                    MM(pb[2 + rs][0:64, :], [(SEL, accS[hh][0:65, blk])], ["cstf", ("accS", hh)], [("vtp", rs)])
                    ACT(rls[rs][0:64, :], pb[2 + rs][0:64, :], AF.Ln, [("vtp", rs)], [("rl", rs)])
                    ACT(rls[rs][0:64, :], rls[rs][0:64, :], AF.Exp, [("rl", rs)], [("rl", rs)], scale=-1.0)
                    TT("dve", aob[64 * hh:64 * hh + 64, blk], accS[hh][0:64, blk], rls[rs][0:64, :], ALU.mult,
                       [("accS", hh), ("rl", rs)], [("aob", hh)])
            ACT(sqa, aob, AF.Square, [("aob", 0), ("aob", 1)], ["sqa"])

            def fn(e):
                ins = None
                for tt in range(NT):
                    ins = e.matmul(pb[3][:, tt:tt + 1], lhsT=sqa[:, tt * 128:(tt + 1) * 128], rhs=ones_bf[:, 0:1], start=True, stop=True)
                return ins
            S.op("pe", fn, ["sqa", "ones_bf"], [("vtp", 1)])
            if ch == 0:
                CP("dve", ssq_a[:], pb[3][:, 0:NT], [("vtp", 1)], ["ssq_a"])
            else:
                TT("dve", ssq_a[:], pb[3][:, 0:NT], ssq_a[:], ALU.add, [("vtp", 1), "ssq_a"], ["ssq_a"])
            DMA("sp", mixs[ch, :, :], aob, "aost", [("aob", 0), ("aob", 1)], [])
        if dbg == "attn":
            return finish([(ssq_a[:], ["ssq_a"]), (ssq_l[:], ["ssq_l"])])

        S.barrier()
        w1 = bf(R1[:]).rearrange("p (k n) -> p k n", k=8)
        wo = bf(R3[:, 0:4096]).rearrange("p (k n) -> p k n", k=8)
        for q8 in range(8):
            fbs = range(q8 * 4, q8 * 4 + 4)
            S.dma_group("pool", [((lambda e, d=w1[:, :, q8 * 512:(q8 + 1) * 512], s_=w_ff1_v[:, :, q8 * 512:(q8 + 1) * 512]: e.dma_start(out=d, in_=s_)),
                                  [], [("w1", fb) for fb in fbs])], ("w1g", q8))
            if q8 in (3, 7):
                load_w2_group(q8)
        NW = 4
        wtmp = [R4[:, i * 1024:(i + 1) * 1024] for i in range(NW)]
        g1_bc = R4[:, 4096:5120]
        g2_bc = R4[:, 5120:6144]
        bscr2 = R4[:, 6144:7168]
        bcast_rows(g1_bc, modc[:, 16:24], "modc", "g1_bc", bscr2)
        bcast_rows(g2_bc, modc[:, 40:48], "modc", "g2_bc", bscr2)
        bcast_rows(fg_bc, pcolA[:, 72:80], "pcolA", "fg_bc", bscr2)
        for ssq_, rstd_, tok in ((ssq_a, rstd_a, "rstd_a"), (ssq_l, rstd_l, "rstd_l")):
            TS("dve", ms1[:], ssq_[:], 1.0 / 512.0, EPS, ALU.mult, ALU.add, ["ssq_a", "ssq_l", "ms1"], ["ms1"])
            TT("pool", rstd_[:], ms1[:], neghalf32[:], ALU.pow, ["ms1", "neghalf32"], [tok])
        for k in range(8):
            s_ = k % NW
            DMA("sp", wtmp[s_], w_out_v[:, k, :], ("wtmp", s_), [], [("wtmp", s_)])
            STT(wo[:, k, :], wtmp[s_], pcolB[:, 44 + k:45 + k], g1_bc, ALU.mult, ALU.mult, [("wtmp", s_), "pcolB", "g1_bc"], ["wo"])
        S.alias(["mg"], [("wtmp", 0)])
        S.alias([("pb", 4)], [("pbb", k) for k in range(4, 8)])
        S.alias([("acc", 1, 1)], [("pbb", k) for k in range(4)])
        S.alias([("x1g", 0, 0), ("x1g", 0, 1)], [("wtmp", 1), ("wtmp", 2)])
        S.alias([("x1g", 1, 0), ("x1g", 1, 1)], [("wtmp", 3), "g1_bc"])
        G = 256
        NG = SEQ // G
        mg = bf(R4[:, 0:1024]).rearrange("p (k t) -> p k t", k=8)
        x1g = [R4[:, 1024 + i * 2048:1024 + (i + 1) * 2048].rearrange("p (t f) -> p t f", t=2) for i in range(2)]
        h2gs = [bf(R3[:, 4096 + i * 1024:4096 + (i + 1) * 1024]).rearrange("p (k t) -> p k t", k=8) for i in range(2)]
        hidb = [bf(R3[:, 6144 + i * 128:6144 + (i + 1) * 128]) for i in range(4)]
        tmpf = [R3[:, 6656 + i * 512:6656 + (i + 1) * 512] for i in range(2)] + [R4[:, 6144 + i * 512:6144 + (i + 1) * 512] for i in range(2)]
        S.alias([("tmpf", 2), ("tmpf", 3)], [("bcs", k) for k in range(8)])
        xs2 = [bf(R3[:, 7680:8192]), bf(R5[:, 192:704])]
        ssq2 = R5[:, 0:32]
        ms2 = R5[:, 32:64]
        rstd2 = R5[:, 64:96]
        ssq3 = R5[:, 96:128]
        ms3 = R5[:, 128:160]
        rstd3 = R5[:, 160:192]
        junk2 = bf(R5[:, 704:1216])
        hid = [R5[:, 2240 + i * 256:2240 + (i + 1) * 256] for i in range(3)]

        def pro_dma(g):
            DMA("sp", mg, mixs_v[:, :, g * G:(g + 1) * G], "mg", [], ["mg"])
            DMA("sp", x1g[g % 2], x[g * G:(g + 1) * G, :].rearrange("(t p) f -> p t f", p=128), ("xg", g % 2), [],
                [("x1g", g % 2, 0), ("x1g", g % 2, 1)])

        def pro_woutA(g, tt, hf):
            tile_i = g * 2 + tt
            cs_ = slice(hf * 512, (hf + 1) * 512)
            ts_ = (tt * 2 + hf) % 2
            MM(pb[4][:], [(mg[:, k, tt * 128:(tt + 1) * 128], wo[:, k, cs_]) for k in range(4)], ["mg", "wo"], [("pb", 4)])
            TS("dve", tmpf[ts_], pb[4][:], rstd_a[:, tile_i:tile_i + 1], None, ALU.mult, None, [("pb", 4), "rstd_a"], [("tmpf", ts_)])

        def pro_woutL(g, tt, hf):
            xb_ = g % 2
            tile_i = g * 2 + tt
            cs_ = slice(hf * 512, (hf + 1) * 512)
            ts_ = (tt * 2 + hf) % 2
            MM(pb[4][:], [(mg[:, k, tt * 128:(tt + 1) * 128], wo[:, k, cs_]) for k in range(4, 8)], ["mg", "wo"], [("pb", 4)])
            STT(tmpf[ts_], pb[4][:], rstd_l[:, tile_i:tile_i + 1], tmpf[ts_], ALU.mult, ALU.add, [("pb", 4), "rstd_l", ("tmpf", ts_)], [("tmpf", ts_)])
            TT("pool", x1g[xb_][:, tt, cs_], tmpf[ts_], x1g[xb_][:, tt, cs_], ALU.add, [("tmpf", ts_), ("x1g", xb_, tt)], [("x1g", xb_, tt)])

        def pro_norm(g, tt):
            xb_ = g % 2
            tile_i = g * 2 + tt
            tok = ("n2", tile_i)
            ACT(junk2, x1g[xb_][:, tt, :], AF.Square, [("x1g", xb_, tt)], ["junk2", ("ssq", tok)], accum=ssq2[:, tile_i:tile_i + 1])
            TS("dve", ms2[:, tile_i:tile_i + 1], ssq2[:, tile_i:tile_i + 1], 1.0 / DM, EPS, ALU.mult, ALU.add, [("ssq", tok)], [("ms", tok)])
            TT("pool", rstd2[:, tile_i:tile_i + 1], ms2[:, tile_i:tile_i + 1], NEGHALF, ALU.pow, [("ms", tok), "cstf"], [("rstd", tok)])
            TS("dve", xs2[tt], x1g[xb_][:, tt, :], rstd2[:, tile_i:tile_i + 1], None, ALU.mult, None, [("rstd", tok), ("x1g", xb_, tt)], [("xs2", tt)])

        def pro_tr(g, tt):
            h2g = h2gs[g % 2]
            pv = bf(pb[4][:]).rearrange("p (k t) -> p k t", k=8)
            TRS([(pv[:, k, :], xs2[tt][:, k * 128:(k + 1) * 128]) for k in range(8)], [("xs2", tt), "ident_bf"], [("pb", 4)])
            for k in range(8):
                TS("dve", h2g[:, k, tt * 128:(tt + 1) * 128], pv[:, k, :], gs2[:, k:k + 1], SH2[:, k:k + 1], ALU.mult, ALU.add,
                   [("pb", 4), "gs2", "modc"], [("h2g", g % 2, tt)])

        def ff1(g, fb):
            h2g = h2gs[g % 2]
            hs = fb % 3
            MM(pb[5 + hs][:, 0:256], [(w1[:, k, fb * 128:(fb + 1) * 128], h2g[:, k, :]) for k in range(8)],
               [("w1", fb), ("h2g", g % 2, 0), ("h2g", g % 2, 1)], [("pb", 5 + hs)])
            ACT(hid[hs], pb[5 + hs][:, 0:256], AF.Relu, [("pb", 5 + hs)], [("hid", hs)])
            TT("pool", hidb[fb % 4], hid[hs], hid[hs], ALU.mult, [("hid", hs)], [("hidb", fb % 4)])

        def ff2(g, fb):
            for tt in range(2):
                for hf in range(2):
                    MM(pb[tt * 2 + hf][:], [(hidb[fb % 4][:, tt * 128:(tt + 1) * 128], w2[:, fb, hf * 512:(hf + 1) * 512])],
                       [("hidb", fb % 4), ("w2", fb)], [("acc", tt, hf)], first_start=(fb == 0), skip=True)

        def epilogue(g):
            xb_ = g % 2
            for tt in range(2):
                tile_i = g * 2 + tt
                for hf in range(2):
                    cs_ = slice(hf * 512, (hf + 1) * 512)
                    ts_ = tt * 2 + hf
                    TT("dve", tmpf[ts_], pb[tt * 2 + hf][:], g2_bc[:, cs_], ALU.mult, [("acc", tt, hf), "g2_bc"], [("tmpf", ts_)])
            for tt in range(2):
                for hf in range(2):
                    cs_ = slice(hf * 512, (hf + 1) * 512)
                    ts_ = tt * 2 + hf
                    TT("dve", x1g[xb_][:, tt, cs_], tmpf[ts_], x1g[xb_][:, tt, cs_], ALU.add,
                       [("tmpf", ts_), ("x1g", xb_, tt)], [("x1g", xb_, tt)])
            for tt in range(2):
                tile_i = g * 2 + tt
                tok = ("n3", tile_i)
                ACT(junk2, x1g[xb_][:, tt, :], AF.Square, [("x1g", xb_, tt)], ["junk2", ("ssq", tok)], accum=ssq3[:, tile_i:tile_i + 1])
                TS("dve", ms3[:, tile_i:tile_i + 1], ssq3[:, tile_i:tile_i + 1], 1.0 / DM, EPS, ALU.mult, ALU.add, [("ssq", tok)], [("ms", tok)])
                TT("pool", rstd3[:, tile_i:tile_i + 1], ms3[:, tile_i:tile_i + 1], NEGHALF, ALU.pow, [("ms", tok), "cstf"], [("rstd", tok)])
                STT(x1g[xb_][:, tt, :], x1g[xb_][:, tt, :], rstd3[:, tile_i:tile_i + 1], fg_bc[:], ALU.mult, ALU.mult,
                    [("x1g", xb_, tt), ("rstd", tok), "fg_bc"], [("x1g", xb_, tt)])
                DMA("sp", y[tile_i * 128:(tile_i + 1) * 128, :], x1g[xb_][:, tt, :], ("yst", xb_), [("x1g", xb_, tt)], [])

        sched = {0: [("dma",)], 2: [("woutA", 0, 0)], 3: [("woutL", 0, 0)], 4: [("woutA", 0, 1)], 5: [("woutL", 0, 1)], 7: [("norm", 0)],
                 8: [("woutA", 1, 0)], 9: [("woutL", 1, 0)], 10: [("woutA", 1, 1)], 11: [("woutL", 1, 1)],
                 13: [("norm", 1)], 17: [("tr", 0)], 23: [("tr", 1)]}

        def run_pro(g, item):
            if item[0] == "dma":
                pro_dma(g)
            elif item[0] == "woutA":
                pro_woutA(g, item[1], item[2])
            elif item[0] == "woutL":
                pro_woutL(g, item[1], item[2])
            elif item[0] == "norm":
                pro_norm(g, item[1])
            else:
                pro_tr(g, item[1])

        for fbk in sorted(sched):
            for item in sched[fbk]:
                run_pro(0, item)
        for g in range(NG):
            ff1(g, 0)
            ff1(g, 1)
            for fb in range(32):
                if fb + 2 < 32:
                    ff1(g, fb + 2)
                ff2(g, fb)
                if g + 1 < NG:
                    for item in sched.get(fb, ()):
                        run_pro(g + 1, item)
            epilogue(g)
        return finish()


def make_consts():
    c = np.zeros((128, 512), np.float32)
    kk = np.arange(128)[:, None]
    qq = np.arange(256)[None, :]
    c[:, 0:256] = np.where((qq >= kk) & (qq <= kk + 128), 0.0, -240000.0)
    psw = np.zeros((128, 128), np.float32)
    invf = np.zeros(128, np.float32)
    sgn = np.zeros(128, np.float32)
    for hb in (0, 64):
        for j in range(8):
            psw[hb + j + 8, hb + j] = 1.0
            psw[hb + j, hb + j + 8] = 1.0
            f = np.float32(500000.0) ** (-np.float32(j) / np.float32(8))
            invf[hb + j] = f
            invf[hb + j + 8] = f
            sgn[hb + j] = -1.0
            sgn[hb + j + 8] = 1.0
    c[:, 256:384] = psw
    c[64, 384:448] = 1.0
    c[:, 448] = invf
    c[:, 449] = sgn
    c[:, 450] = -0.5
    return c


_CACHE = {}


def make_in_maps(inputs):
    f = lambda a: np.ascontiguousarray(np.asarray(a))
    shared = {
        "w_ada": f(inputs["w_ada"][0]), "b_ada": f(inputs["b_ada"][0]).reshape(48, 128),
        "norm1_g": f(inputs["norm1_g"][0]).reshape(8, 128), "norm2_g": f(inputs["norm2_g"][0]).reshape(8, 128),
        "final_g": f(inputs["final_g"]).reshape(8, 128), "w_in": f(inputs["w_in"][0]),
        "conv_w": f(inputs["conv_w"][0]).reshape(16, 128), "conv_b": f(inputs["conv_b"][0]).reshape(4, 128),
        "lru_wa": f(inputs["lru_wa"][0]).reshape(16, 64, 64), "lru_wx": f(inputs["lru_wx"][0]).reshape(16, 64, 64),
        "lru_ba": f(inputs["lru_ba"][0]).reshape(8, 128), "lru_bx": f(inputs["lru_bx"][0]).reshape(8, 128),
        "lru_lam": f(inputs["lru_lam"][0]).reshape(8, 128), "attn_out_g": f(inputs["attn_out_g"][0]).reshape(4, 128),
        "lru_out_g": f(inputs["lru_out_g"][0]).reshape(4, 128), "w_out": f(inputs["w_out"][0]),
        "w_ff1": f(inputs["w_ff1"][0]), "w_ff2": f(inputs["w_ff2"][0]), "cst": make_consts(),
    }
    maps = []
    for b in range(8):
        m = dict(shared)
        m["x"] = f(inputs["x"][b])
        m["crow"] = f(inputs["c"][b]).reshape(8, 128)
        m["pos"] = f(inputs["positions"][b]).reshape(1, SEQ).astype(np.int32)
        maps.append(m)
    return maps


def kernel(**inputs):
    if "nc" not in _CACHE:
        _CACHE["nc"] = build_program()
    nc = _CACHE["nc"]
    in_maps = make_in_maps(inputs)
    res = run_bass_kernel_spmd(nc, in_maps, core_ids=list(range(8)))
    return np.stack([np.asarray(r["y"]) for r in res.results], axis=0).astype(np.float32)
```

```python
import math
from contextlib import ExitStack

import numpy as np
import concourse.bass as bass
import concourse.mybir as mybir
from concourse.bass_utils import run_bass_kernel_spmd

F32 = mybir.dt.float32
BF16 = mybir.dt.bfloat16
I32 = mybir.dt.int32
AF = mybir.ActivationFunctionType
ALU = mybir.AluOpType

SEQ = 4096
DM = 1024
NT = SEQ // 128
EPS = 1e-6
PI = math.pi
TWO_PI = 2.0 * math.pi
PI_C = 3.1415925
DILS = (1, 4, 16)


class Sched:
    ENGS = ("pe", "act", "dve", "pool", "sp")

    def __init__(self, nc, stack):
        self.nc = nc
        self.stack = stack
        self.ops = {e: [] for e in self.ENGS}
        self.clock_sem = {e: stack.enter_context(nc.semaphore("clk_" + e)) for e in ("pe", "act", "dve", "pool")}
        self.clock = {e: 0 for e in self.clock_sem}
        self.seen = {e: {} for e in self.ENGS}
        self.lastw = {}
        self.readers = {}
        self.dma_sems = {}

    def dma_sem(self, key):
        if key not in self.dma_sems:
            self.dma_sems[key] = [self.stack.enter_context(self.nc.semaphore("dma_%d" % len(self.dma_sems))), 0]
        return self.dma_sems[key]

    def _deps(self, eng, reads, writes, waiter=None):
        waiter = waiter or eng
        need = {}

        def add(ev):
            if ev is None:
                return
            if need.get(ev[0], 0) < ev[1]:
                need[ev[0]] = ev[1]
        for t in reads:
            add(self.lastw.get(t))
        for t in writes:
            w = self.lastw.get(t)
            if w is not None and (w[2] != eng or eng in ("act", "dve", "pool")):
                add(w)
            for r in self.readers.get(t, ()):
                if r[2] != eng:
                    add(r)
        waits = []
        for sem, val in need.items():
            if self.seen[waiter].get(sem, 0) < val:
                self.seen[waiter][sem] = val
                waits.append((sem, val))
        return waits

    def _commit(self, ev, reads, writes):
        for t in reads:
            self.readers.setdefault(t, []).append(ev)
        for t in writes:
            self.lastw[t] = ev
            self.readers[t] = []

    def op(self, eng, fn, reads=(), writes=()):
        waits = self._deps(eng, reads, writes)
        self.clock[eng] += 1
        sem = self.clock_sem[eng]
        ev = (sem, self.clock[eng], eng)
        self._commit(ev, reads, writes)
        self.ops[eng].append((waits, fn, sem, 1))
        return ev

    def dma(self, queue, fn, key, reads=(), writes=()):
        self.ndma = getattr(self, "ndma", 0) + 1
        waits = self._deps("dma#%d" % self.ndma, reads, writes, waiter=queue)
        s = self.dma_sem(key)
        s[1] += 16
        ev = (s[0], s[1], "dma")
        self._commit(ev, reads, writes)
        self.ops[queue].append((waits, fn, s[0], 16))
        return ev

    def dma_group(self, queue, items, key):
        ev = None
        toks = []
        for fn, reads, writes in items:
            ev = self.dma(queue, fn, key, reads, writes)
            toks += list(writes)
        for t in toks:
            self.lastw[t] = ev
        return ev

    def alias(self, new_tokens, old_tokens):
        evs = []
        for t in old_tokens:
            if self.lastw.get(t) is not None:
                evs.append(self.lastw[t])
            evs += list(self.readers.get(t, ()))
        for t in new_tokens:
            self.lastw[t] = None
            self.readers[t] = [(e[0], e[1], "alias") for e in evs]

    def wait_all(self, eng, evs):
        waits = []
        for ev in evs:
            if self.seen[eng].get(ev[0], 0) < ev[1]:
                self.seen[eng][ev[0]] = ev[1]
                waits.append((ev[0], ev[1]))
        if waits:
            self.ops[eng].append((waits, None, None, 0))

    def barrier(self):
        evs = [(self.clock_sem[e], self.clock[e]) for e in self.clock_sem if self.clock[e] > 0]
        evs += [(s[0], s[1]) for s in self.dma_sems.values() if s[1] > 0]
        for e in self.ENGS:
            self.wait_all(e, evs)

    def emit(self, block):
        def run(name):
            def body(e):
                for waits, fn, sem, inc in self.ops[name]:
                    for (s, v) in waits:
                        e.wait_ge(s, v)
                    if fn is not None:
                        fn(e).then_inc(sem, inc)
            return body
        block.tensor(run("pe"))
        block.scalar(run("act"))
        block.vector(run("dve"))
        block.gpsimd(run("pool"))
        block.sync(run("sp"))


def build_program(dbg=None):
    nc = bass.Bass("TRN2", target_bir_lowering=False)

    def din(name, shape, dt=F32):
        return nc.dram_tensor(name, list(shape), dt, kind="ExternalInput").ap()

    x = din("x", [SEQ, DM])
    crow = din("crow", [8, 128])
    pos = din("pos", [1, SEQ], I32)
    w_ada = din("w_ada", [DM, 6 * DM])
    b_ada = din("b_ada", [48, 128])
    norm1_g = din("norm1_g", [8, 128])
    norm2_g = din("norm2_g", [8, 128])
    final_g = din("final_g", [8, 128])
    w_in = din("w_in", [DM, 2560])
    conv_w = din("conv_w", [16, 128])
    conv_b = din("conv_b", [4, 128])
    lru_wa = din("lru_wa", [16, 64, 64])
    lru_wx = din("lru_wx", [16, 64, 64])
    lru_ba = din("lru_ba", [8, 128])
    lru_bx = din("lru_bx", [8, 128])
    lru_lam = din("lru_lam", [8, 128])
    attn_out_g = din("attn_out_g", [4, 128])
    lru_out_g = din("lru_out_g", [4, 128])
    w_out = din("w_out", [DM, DM])
    w_ff1 = din("w_ff1", [DM, 4 * DM])
    w_ff2 = din("w_ff2", [4 * DM, DM])
    cst = din("cst", [128, 512])
    y = nc.dram_tensor("y", [SEQ, DM], F32, kind="ExternalOutput").ap()
    mixs = nc.dram_tensor("mixs", [8, 128, SEQ], BF16, kind="ExternalOutput" if dbg else "Internal").ap()
    if dbg:
        dbgf = nc.dram_tensor("dbgf", [128, 16384], F32, kind="ExternalOutput").ap()
        dbgb = nc.dram_tensor("dbgb", [128, 65536], BF16, kind="ExternalOutput").ap()

    w_in_v = w_in.rearrange("(k p) n -> p k n", p=128)
    w_ada_v = w_ada.rearrange("(k p) n -> p k n", p=128)
    w_out_v = w_out.rearrange("(k p) n -> p k n", p=128)
    w_ff1_v = w_ff1.rearrange("(k p) n -> p k n", p=128)
    w_ff2_v = w_ff2.rearrange("(k p) n -> p k n", p=128)
    mixs_v = mixs.rearrange("k p t -> p k t")

    with ExitStack() as st:
        S = Sched(nc, st)

        def sb(name, shape, dt):
            return st.enter_context(nc.sbuf_tensor(name, list(shape), dt))

        ident_f = sb("ident_f", [128, 128], F32)
        ident_bf = sb("ident_bf", [128, 128], BF16)
        ones_f = sb("ones_f", [128, 128], F32)
        ones_bf = sb("ones_bf", [128, 128], BF16)
        cstf = sb("cstf", [128, 512], F32)
        maskb = sb("maskb", [128, 256], BF16)
        psw = sb("psw", [128, 128], BF16)
        pcolA = sb("pcolA", [128, 80], F32)
        pcolB = sb("pcolB", [128, 64], F32)
        silu_c = sb("silu_c", [128, 8], F32)
        modc = sb("modc", [128, 48], F32)
        gs1 = sb("gs1", [128, 8], F32)
        gs2 = sb("gs2", [128, 8], F32)
        cs = sb("cs", [128, 8], F32)
        cst1 = sb("cst1", [128, 8], F32)
        cst2 = sb("cst2", [128, 8], F32)
        ssq1 = sb("ssq1", [128, NT], F32)
        ms1 = sb("ms1", [128, NT], F32)
        rstd1 = sb("rstd1", [128, NT], F32)
        ssq_a = sb("ssq_a", [128, NT], F32)
        ssq_l = sb("ssq_l", [128, NT], F32)
        rstd_a = sb("rstd_a", [128, NT], F32)
        rstd_l = sb("rstd_l", [128, NT], F32)
        colt = sb("colt", [128, 8], F32)
        neghalf32 = sb("neghalf32", [128, NT], F32)

        R1 = sb("R1", [128, 16384], F32)
        R2 = sb("R2", [128, 16384], F32)
        R3 = sb("R3", [128, 8192], F32)
        R4 = sb("R4", [128, 7168], F32)
        prowA = R4[0:80, 3584:3712]
        prowB = R4[0:64, 3712:3840]
        R5 = sb("R5", [128, 3072], F32)
        fg_bc = R5[:, 1216:2240]
        wablk = R5[:, 0:1024].bitcast(BF16).rearrange("p (i n) -> p i n", i=16)

        pb = [st.enter_context(nc.psum_tensor("pb%d" % i, [128, 512], F32)) for i in range(8)]

        def bf(ap):
            return ap.bitcast(BF16)

        MASK = cstf[:, 0:256]
        SEL = cstf[0:65, 384:448]
        INVF = cstf[:, 448:449]
        SGN = cstf[:, 449:450]
        NEGHALF = cstf[:, 450:451]

        def ACT(out, in_, func, reads, writes, bias=None, scale=None, accum=None):
            kw = {}
            if bias is not None:
                kw["bias"] = bias
            if scale is not None:
                kw["scale"] = scale
            if accum is not None:
                kw["accum_out"] = accum
            return S.op("act", lambda e: e.activation(out=out, in_=in_, func=func, **kw), reads, writes)

        def TS(eng, out, in0, s1, s2, op0, op1, reads, writes):
            if op1 is None:
                return S.op(eng, lambda e: e.tensor_scalar(out=out, in0=in0, scalar1=s1, scalar2=None, op0=op0), reads, writes)
            return S.op(eng, lambda e: e.tensor_scalar(out=out, in0=in0, scalar1=s1, scalar2=s2, op0=op0, op1=op1), reads, writes)

        def STT(out, in0, scalar, in1, op0, op1, reads, writes):
            return S.op("dve", lambda e: e.scalar_tensor_tensor(out=out, in0=in0, scalar=scalar, in1=in1, op0=op0, op1=op1), reads, writes)

        def TT(eng, out, in0, in1, op, reads, writes):
            return S.op(eng, lambda e: e.tensor_tensor(out=out, in0=in0, in1=in1, op=op), reads, writes)

        def CP(eng, out, in_, reads, writes):
            return S.op(eng, lambda e: e.tensor_copy(out=out, in_=in_), reads, writes)

        def MM(out, pairs, reads, writes, first_start=True, skip=False):
            def fn(e):
                ins = None
                n = len(pairs)
                for i, (l, r) in enumerate(pairs):
                    ins = e.matmul(out, lhsT=l, rhs=r, start=(first_start and i == 0), stop=(i == n - 1),
                                   skip_group_check=skip)
                return ins
            return S.op("pe", fn, reads, writes)

        def TRS(items, reads, writes):
            def fn(e):
                ins = None
                for o, i in items:
                    ins = e.transpose(out=o, in_=i, identity=ident_bf[:])
                return ins
            return S.op("pe", fn, reads, writes)

        def DMA(queue, out, in_, key, reads, writes):
            return S.dma(queue, lambda e: e.dma_start(out=out, in_=in_), key, reads, writes)

        def finish(dumps=()):
            evs = []
            fo = bo = 0
            for ap, toks in dumps:
                n = ap.shape[-1]
                if ap.dtype == F32:
                    evs.append(DMA("sp", dbgf[:, fo:fo + n], ap, "dbg", toks, []))
                    fo += n
                else:
                    evs.append(DMA("sp", dbgb[:, bo:bo + n], ap, "dbg", toks, []))
                    bo += n
            S.barrier()
            with nc.Block() as block:
                S.emit(block)
            return nc

        S.op("pool", lambda e: e.memset(ident_f[:], 1.0), [], ["ident_f"])
        S.op("pool", lambda e: e.affine_select(out=ident_f[:], in_=ident_f[:], pattern=[[-1, 128]],
                                               compare_op=ALU.is_equal, fill=0.0, base=0, channel_multiplier=1),
             ["ident_f"], ["ident_f"])
        S.op("pool", lambda e: e.memset(ones_f[:], 1.0), [], ["ones_f"])
        S.op("pool", lambda e: e.memset(ones_bf[:], 1.0), [], ["ones_bf"])
        S.op("pool", lambda e: e.memset(prowB, 0.0), [], ["prowB"])
        S.op("pool", lambda e: e.memset(neghalf32[:], -0.5), [], ["neghalf32"])
        CP("pool", ident_bf[:], ident_f[:], ["ident_f"], ["ident_bf"])
        DMA("sp", cstf[:], cst, "cst", [], ["cstf"])
        CP("dve", maskb[:], cstf[:, 0:256], ["cstf"], ["maskb"])
        CP("dve", psw[:], cstf[:, 256:384], ["cstf"], ["psw"])

        items = []
        for (dst, src) in ((prowA[0:48, :], b_ada), (prowA[48:56, :], crow), (prowA[56:64, :], norm1_g),
                           (prowA[64:72, :], norm2_g), (prowA[72:80, :], final_g)):
            items.append(((lambda e, d=dst, s=src: e.dma_start(out=d, in_=s)), [], ["prowA"]))
        S.dma_group("sp", items, "prowA")
        items = []
        for (dst, src) in ((prowB[0:16, :], conv_w), (prowB[16:20, :], conv_b), (prowB[20:28, :], lru_ba),
                           (prowB[28:36, :], lru_bx), (prowB[36:44, :], lru_lam), (prowB[44:48, :], attn_out_g),
                           (prowB[48:52, :], lru_out_g)):
            items.append(((lambda e, d=dst, s=src: e.dma_start(out=d, in_=s)), ["prowB"], ["prowB"]))
        S.dma_group("sp", items, "prowB")
        MM(pb[5][:, 0:80], [(prowA, ident_f[0:80, 0:80])], ["prowA", "ident_f"], ["pb0"])
        ACT(pcolA[:], pb[5][:, 0:80], AF.Copy, ["pb0"], ["pcolA"])
        MM(pb[6][:, 0:64], [(prowB, ident_f[0:64, 0:64])], ["prowB", "ident_f"], ["pb1"])
        ACT(pcolB[:], pb[6][:, 0:64], AF.Copy, ["pb1"], ["pcolB"])
        ACT(silu_c[:], pcolA[:, 48:56], AF.Silu, ["pcolA"], ["silu_c"])

        wab = [r_.rearrange("p (k n) -> p k n", k=8) for r_ in
               (R3[:, 0:4096], R3[:, 4096:8192], R1[:, 0:4096], R1[:, 4096:8192], R1[:, 8192:12288], R1[:, 12288:16384])]
        xts = [R4[:, i * 1024:(i + 1) * 1024] for i in range(3)]
        junk = bf(R4[:, 3072:3584])
        xss = [bf(R2[:, t * 512:(t + 1) * 512]) for t in range(NT)]

        def rms_tile(src_f32, ssq_col, ms_col, rstd_col, tok, extra_reads):
            ACT(junk, src_f32, AF.Square, extra_reads, ["junk", ("ssq", tok)], accum=ssq_col)
            TS("dve", ms_col, ssq_col, 1.0 / DM, EPS, ALU.mult, ALU.add, [("ssq", tok)], [("ms", tok)])
            TT("pool", rstd_col, ms_col, NEGHALF, ALU.pow, [("ms", tok), "cstf"], [("rstd", tok)])

        def stage1a(t):
            s3 = t % 3
            DMA("sp", xts[s3], x[t * 128:(t + 1) * 128, :], ("xt", s3), [], [("xt", s3)])
            rms_tile(xts[s3], ssq1[:, t:t + 1], ms1[:, t:t + 1], rstd1[:, t:t + 1], ("n1", t), [("xt", s3)])
            TS("dve", xss[t], xts[s3], rstd1[:, t:t + 1], None, ALU.mult, None, [("rstd", ("n1", t)), ("xt", s3)], [("xs", t)])

        def ada_buf(blk):
            return (2 + blk) if blk < 4 else (blk % 2)

        def ada_dma(blk):
            b_ = ada_buf(blk)
            DMA("sp", wab[b_], w_ada_v[:, :, blk * 512:(blk + 1) * 512], ("wab", b_), [], [("wab", b_)])

        S.alias(["pbm1"], ["pb1"])

        def ada_mm(blk):
            b_ = ada_buf(blk)
            bank = pb[6] if blk < 4 else pb[7]

            def fn(e, b_=b_, blk=blk, bank=bank):
                ins = None
                for jj in range(4):
                    j = blk * 4 + jj
                    for k in range(8):
                        ins = e.matmul(bank[:, j:j + 1], lhsT=wab[b_][:, k, jj * 128:(jj + 1) * 128],
                                       rhs=silu_c[:, k:k + 1], start=(k == 0), stop=(k == 7))
                return ins
            S.op("pe", fn, [("wab", b_), "silu_c"], ["pbm1" if blk < 4 else "pb2"])

        hT = bf(R1[:]).rearrange("p (k t) -> p k t", k=8)
        SH1 = modc[:, 0:8]
        SH2 = modc[:, 24:32]

        def stage1b(g4):
            pvs = [bf(pb[b_][:]).rearrange("p (k t) -> p k t", k=2) for b_ in range(4)]
            for b_ in range(4):
                TRS([(pvs[b_][:, kk, ti * 128:(ti + 1) * 128], xss[g4 * 4 + ti][:, (2 * b_ + kk) * 128:(2 * b_ + kk + 1) * 128])
                     for kk in range(2) for ti in range(4)], [("xs", g4 * 4 + ti) for ti in range(4)] + ["ident_bf"], [("pb", b_)])
            for k in range(8):
                b_, kk = k // 2, k % 2
                dst = hT[:, k, g4 * 512:(g4 + 1) * 512]
                if b_ % 2:
                    ACT(dst, pvs[b_][:, kk, :], AF.Identity, [("pb", b_), "gs1", "modcA"], [("hTa", g4)],
                        bias=SH1[:, k:k + 1], scale=gs1[:, k:k + 1])
                else:
                    TS("dve", dst, pvs[b_][:, kk, :], gs1[:, k:k + 1], SH1[:, k:k + 1], ALU.mult, ALU.add,
                       [("pb", b_), "gs1", "modcA"], [("hTd", g4)])

        tnext = 0
        gnext = 0
        for blk in range(6):
            ada_dma(blk)
        for blk in range(12):
            ada_mm(blk)
            if blk == 3:
                TT("dve", modc[:, 0:16], pb[6][:, 0:16], pcolA[:, 0:16], ALU.add, ["pbm1", "pcolA"], ["modcA"])
                STT(gs1[:], modc[:, 8:16], 1.0, pcolA[:, 56:64], ALU.add, ALU.mult, ["modcA", "pcolA"], ["gs1"])
                S.alias([("hTa", g) for g in range(8)] + [("hTd", g) for g in range(8)], [("wab", i) for i in range(2, 6)])
            for _ in range(3):
                if tnext < NT:
                    stage1a(tnext)
                    tnext += 1
            if 4 <= blk and blk + 2 < 12:
                ada_dma(blk + 2)
            if blk >= 4 and gnext < NT // 4 and tnext >= 4 * (gnext + 1):
                stage1b(gnext)
                gnext += 1
        while tnext < NT:
            stage1a(tnext)
            tnext += 1
        while gnext < NT // 4:
            stage1b(gnext)
            gnext += 1
        TT("dve", modc[:, 16:48], pb[7][:, 16:48], pcolA[:, 16:48], ALU.add, ["pb2", "pcolA"], ["modc"])
        STT(gs2[:], modc[:, 32:40], 1.0, pcolA[:, 64:72], ALU.add, ALU.mult, ["modc", "pcolA"], ["gs2"])

        ACT(cst1[:], pcolB[:, 36:44], AF.Exp, ["pcolB"], ["cst1"], scale=-1.0)
        TS("dve", cst2[:], cst1[:], -0.25, 1.0 / 3.0, ALU.mult, ALU.add, ["cst1"], ["cst2"])
        TT("dve", cst2[:], cst2[:], cst1[:], ALU.mult, ["cst2", "cst1"], ["cst2"])
        TS("dve", cst2[:], cst2[:], -1.0, 0.5, ALU.mult, ALU.add, ["cst2"], ["cst2"])
        TT("dve", cst2[:], cst2[:], cst1[:], ALU.mult, ["cst2", "cst1"], ["cst2"])
        TS("dve", cst2[:], cst2[:], -1.0, 1.0, ALU.mult, ALU.add, ["cst2"], ["cst2"])
        TT("dve", cst2[:], cst2[:], cst1[:], ALU.mult, ["cst2", "cst1"], ["cst2"])
        TS("dve", cs[:], cst2[:], -8.0, None, ALU.mult, None, ["cst2"], ["cs"])

        def bcast_rows(dst, col_ap, col_tok, dst_tok, scratch):
            for k in range(8):
                TS("dve", scratch[:, k * 128:(k + 1) * 128], ident_f[:], col_ap[:, k:k + 1], None, ALU.mult, None,
                   ["ident_f", col_tok], [("bcs", k)])
                MM(pb[3 + (k // 4)][:, (k % 4) * 128:(k % 4 + 1) * 128], [(ones_f[:], scratch[:, k * 128:(k + 1) * 128])],
                   [("bcs", k), "ones_f"], [("pbb", k)])
            for hh in range(2):
                ACT(dst[:, hh * 512:(hh + 1) * 512], pb[3 + hh][:, :], AF.Copy, [("pbb", 4 * hh + i) for i in range(4)], [dst_tok])


        if dbg == "hT":
            return finish([(bf(R1[:]), [("hTa", t) for t in range(8)] + [("hTd", t) for t in range(8)]), (modc[:], ["modc"]), (pcolB[:], ["pcolB"]), (cs[:], ["cs"])])

        hT_blk = lambda n: [("hTa", n), ("hTd", n)]

        S.barrier()
        HL = 2048
        T1 = R2[:, 0:4096]
        T2 = R2[:, 4096:8192]
        T8 = R2[:, 8192:12288]
        TA = [R2[:, 12288:14336], R2[:, 14336:16384]]
        T3 = bf(R3[:, 0:2048])
        T4 = bf(R3[:, 2048:4096])
        TB = [R3[:, 4096:6144], R3[:, 6144:8192]]
        wl = [bf(R4[:, i * 1024:(i + 1) * 1024]).rearrange("p (k n) -> p k n", k=8) for i in range(2)]
        recb = [bf(R4[:, 2048 + i * 1024:2048 + (i + 1) * 1024]) for i in range(2)]
        sqb = bf(R4[:, 4096:5120])
        TC = R4[:, 5120:7168]
        T9 = R5[:, 1024:3072]
        carry = colt[:, 0:1]
        halfb = sb("halfb", [128, 16], F32)
        hcs = sb("hcs", [128, 8], F32)
        S.op("pool", lambda e: e.memset(wablk, 0.0), [], ["wablk"])
        items = []
        for d_ in range(2):
            for gi, wsrc in enumerate((lru_wa, lru_wx)):
                for n_ in range(8):
                    idx = (d_ * 2 + gi) * 4 + n_ // 2
                    o_ = (n_ % 2) * 64
                    items.append(((lambda e, dd=wablk[o_:o_ + 64, idx, o_:o_ + 64], ss=wsrc[d_ * 8 + n_]: e.dma_start(out=dd, in_=ss)),
                                  ["wablk"], ["wablk"]))
        S.dma_group("pool", items, "wablk")
        TS("dve", halfb[:], pcolB[:, 20:36], 0.5, None, ALU.mult, None, ["pcolB"], ["halfb"])
        TS("dve", hcs[:], cs[:], 0.5, None, ALU.mult, None, ["cs"], ["hcs"])

        pbrot = [0]

        def next_pb(lo=0, n=4):
            i = lo + pbrot[0] % n
            pbrot[0] += 1
            return i

        def load_wl(c):
            cb = c % 2
            S.dma_group("pool", [
                ((lambda e, d=wl[cb][:, :, 0:128], s_=w_in_v[:, :, 1536 + c * 128:1536 + (c + 1) * 128]: e.dma_start(out=d, in_=s_)), [], [("wl", cb)]),
                ((lambda e, d=wl[cb][:, :, 128:256], s_=w_in_v[:, :, 2048 + c * 128:2048 + (c + 1) * 128]: e.dma_start(out=d, in_=s_)), [], [("wl", cb)]),
            ], ("wl", cb))

        def proj_xr(c):
            cb = c % 2
            for n in range(8):
                pi = next_pb()
                MM(pb[pi][:], [(wl[cb][:, k, 0:128], hT[:, k, n * 512:(n + 1) * 512]) for k in range(8)],
                   [("wl", cb)] + hT_blk(n), [("pb", pi)])
                ACT(T1[:, n * 512:(n + 1) * 512], pb[pi][:], AF.Copy, [("pb", pi)], ["T1"])

        def proj_gr(c):
            cb = c % 2
            for n in range(8):
                pi = next_pb()
                MM(pb[pi][:], [(wl[cb][:, k, 128:256], hT[:, k, n * 512:(n + 1) * 512]) for k in range(8)],
                   [("wl", cb)] + hT_blk(n), [("pb", pi)])
                ACT(T4[:, n * 512:(n + 1) * 512], pb[pi][:], AF.Gelu_apprx_tanh, [("pb", pi)], ["T4"])

        def conv(c):
            ACT(T2, T1, AF.Identity, ["T1", "pcolB"], ["T2"], bias=pcolB[:, 16 + c:17 + c], scale=pcolB[:, 8 + c:9 + c])
            STT(T2[:, 2:SEQ], T1[:, 0:SEQ - 2], pcolB[:, c:c + 1], T2[:, 2:SEQ], ALU.mult, ALU.add, ["T1", "T2", "pcolB"], ["T2"])
            STT(T2[:, 1:SEQ], T1[:, 0:SEQ - 1], pcolB[:, 4 + c:5 + c], T2[:, 1:SEQ], ALU.mult, ALU.add, ["T1", "T2", "pcolB"], ["T2"])
            STT(T2[:, 0:SEQ - 1], T1[:, 1:SEQ], pcolB[:, 12 + c:13 + c], T2[:, 0:SEQ - 1], ALU.mult, ALU.add, ["T1", "T2", "pcolB"], ["T2"])
            CP("dve", T3, T2, ["T2"], ["T3"])

        piece_ctr = [0]

        def gates_scan(c, d_, hh):
            lo = hh * HL
            bi_ = piece_ctr[0] % 2
            piece_ctr[0] += 1
            A, Bf = TA[bi_], TB[bi_]
            ta, tb = ("TA", bi_), ("TB", bi_)
            for gi, (Tg, tg) in enumerate(((A, ta), (Bf, tb))):
                bcol = d_ * 4 + c + (0 if gi == 0 else 8)
                for nb in range(HL // 512):
                    pi = next_pb()
                    MM(pb[pi][:], [(wablk[:, (d_ * 2 + gi) * 4 + c, :], T3[:, lo + nb * 512:lo + (nb + 1) * 512])],
                       ["wablk", "T3"], [("pb", pi)])
                    ACT(Tg[:, nb * 512:(nb + 1) * 512], pb[pi][:], AF.Tanh, [("pb", pi), "halfb"], [tg],
                        bias=halfb[:, bcol:bcol + 1], scale=0.5)
            ci = d_ * 4 + c
            ACT(TC, A, AF.Exp, [ta, "cs"], ["TC"], bias=cs[:, ci:ci + 1], scale=cs[:, ci:ci + 1])
            ACT(A, A, AF.Exp, [ta, "hcs"], [ta], bias=hcs[:, ci:ci + 1], scale=hcs[:, ci:ci + 1])
            ACT(TC, TC, AF.Sqrt, ["TC"], ["TC"], bias=0.25, scale=-0.25)
            STT(Bf, Bf, 1.0, T2[:, lo:lo + HL], ALU.add, ALU.mult, [tb, "T2"], [tb])
            TT("dve", Bf, Bf, TC, ALU.mult, [tb, "TC"], [tb])
            if d_ == 0:
                init = 0.0 if hh == 0 else T8[:, HL - 1:HL]
                S.op("dve", lambda e: e.tensor_tensor_scan(out=T8[:, lo:lo + HL], data0=A, data1=Bf, initial=init,
                                                            op0=ALU.mult, op1=ALU.add), [ta, tb, "T8"], ["T8"])
            else:
                init = 0.0 if hh == 1 else carry
                S.op("dve", lambda e: e.tensor_tensor_scan(out=T9[:, ::-1], data0=A[:, ::-1], data1=Bf[:, ::-1], initial=init,
                                                            op0=ALU.mult, op1=ALU.add), [ta, tb, "carry", "T9"], ["T9"])
                if hh == 1:
                    CP("dve", carry, T9[:, 0:1], ["T9"], ["carry"])

        def combine(c, hh):
            lo = hh * HL
            rb = recb[hh]
            TT("dve", TC, T8[:, lo:lo + HL], T9, ALU.add, ["T8", "T9", "TC"], ["TC"])
            TT("dve", rb, TC, T4[:, lo:lo + HL], ALU.mult, ["TC", "T4"], [("recb", hh)])
            ACT(sqb, rb, AF.Square, [("recb", hh)], ["sqb"])

            def fn(e):
                ins = None
                for tt in range(HL // 128):
                    ins = e.matmul(pb[4][:, hh * 16 + tt:hh * 16 + tt + 1], lhsT=sqb[:, tt * 128:(tt + 1) * 128],
                                   rhs=ones_bf[:, 0:1], start=True, stop=True)
                return ins
            S.op("pe", fn, ["sqb", "ones_bf"], [("pb", 4)])
            if c == 0:
                CP("dve", ssq_l[:, hh * 16:(hh + 1) * 16], pb[4][:, hh * 16:(hh + 1) * 16], [("pb", 4)], ["ssq_l"])
            else:
                TT("dve", ssq_l[:, hh * 16:(hh + 1) * 16], pb[4][:, hh * 16:(hh + 1) * 16], ssq_l[:, hh * 16:(hh + 1) * 16],
                   ALU.add, [("pb", 4), "ssq_l"], ["ssq_l"])
            DMA("sp", mixs[4 + c, :, lo:lo + HL], rb, ("recst", hh), [("recb", hh)], [])

        load_wl(0)
        proj_xr(0)
        for c in range(4):
            if c + 1 < 4:
                load_wl(c + 1)
            conv(c)
            proj_gr(c)
            gates_scan(c, 0, 0)
            if c + 1 < 4:
                proj_xr(c + 1)
            gates_scan(c, 0, 1)
            gates_scan(c, 1, 1)
            combine(c, 1)
            gates_scan(c, 1, 0)
            combine(c, 0)
        if dbg == "lru":
            return finish([(ssq_l[:], ["ssq_l"])])

        S.barrier()
        qT = bf(R2[:, 0:8192]).rearrange("p (k t) -> p k t", k=4)
        kT = bf(R2[:, 8192:16384]).rearrange("p (k t) -> p k t", k=4)
        vT = bf(R3[:]).rearrange("p (k t) -> p k t", k=4)
        wq = bf(R4[:, 0:4096]).rearrange("p (k n) -> p k n", k=8)
        posi = R4[:, 4096:4608].bitcast(I32)
        ang = R4[:, 4608:5120]
        kint = R4[:, 5120:5632].bitcast(I32)
        kfl = R4[:, 5632:6144]
        Ctabs = [R4[:, 6144:6656], R3[:, 0:512]]
        Stabs = [R4[:, 6656:7168], R3[:, 512:1024]]
        th = R5[:, 2560:3072]
        qbt = [bf(R5[:, 2048 + i * 256:2048 + (i + 1) * 256]) for i in range(2)]
        t1s = [R5[:, i * 512:(i + 1) * 512] for i in range(2)]
        t2s = [R5[:, 1024 + i * 512:1024 + (i + 1) * 512] for i in range(2)]

        S.dma_group("pool", [((lambda e, d=wq[:, k, :], s_=w_in_v[:, k, 0:1024]: e.dma_start(out=d, in_=s_)), [], ["wq"]) for k in range(8)], "wq")
        def rot_rest(n, j, pi, sl, blk):
            pj = 4 + pi
            Ctab, Stab = Ctabs[n % 2], Stabs[n % 2]
            MM(pb[pj][:], [(psw[:], qbt[sl])], ["psw", ("qbt", sl)], [("pb", pj)])
            TT("dve", t1s[sl], pb[pi][:], Ctab, ALU.mult, [("pb", pi), ("Ctab", n % 2), ("qbt", sl)], [("t1", sl)])
            TT("dve", t2s[sl], pb[pj][:], Stab, ALU.mult, [("pb", pj), ("Stab", n % 2)], [("t2", sl)])
            dst = (qT if j < 4 else kT)[:, j % 4, blk]
            TT("pool", dst, t1s[sl], t2s[sl], ALU.add, [("t1", sl), ("t2", sl)], [("qk", j)])

        def build_tables(n):
            blk = slice(n * 512, (n + 1) * 512)
            Ctab, Stab = Ctabs[n % 2], Stabs[n % 2]
            DMA("sp", posi, pos[0:1, blk].partition_broadcast(128), "posi", [], ["posi"])
            CP("dve", ang, posi, ["posi"], ["ang"])
            TS("dve", ang, ang, INVF, None, ALU.mult, None, ["ang", "cstf"], ["ang"])
            for which in range(2):
                if which == 1:
                    TS("dve", ang, ang, PI / 2, None, ALU.add, None, ["ang"], ["ang"])
                TS("dve", kint, ang, 1.0 / TWO_PI, None, ALU.mult, None, ["ang"], ["kint"])
                CP("dve", kfl, kint, ["kint"], ["kfl"])
                STT(th, kfl, -TWO_PI, ang, ALU.mult, ALU.add, ["kfl", "ang"], ["th"])
                TS("dve", th, th, -PI_C, PI_C, ALU.max, ALU.min, ["th"], ["th"])
                if which == 0:
                    ACT(Stab, th, AF.Sin, ["th", "cstf"], [("Stab", n % 2)], scale=SGN)
                else:
                    ACT(Ctab, th, AF.Sin, ["th"], [("Ctab", n % 2)])

        pend = None
        build_tables(0)
        for n in range(8):
            blk = slice(n * 512, (n + 1) * 512)
            for j in range(8):
                pi = next_pb()
                sl = (n * 8 + j) % 2
                MM(pb[pi][:], [(wq[:, k, j * 128:(j + 1) * 128], hT[:, k, blk]) for k in range(8)], ["wq"] + hT_blk(n), [("pb", pi)])
                ACT(qbt[sl], pb[pi][:], AF.Copy, [("pb", pi)], [("qbt", sl)])
                if pend is not None:
                    rot_rest(*pend)
                pend = (n, j, pi, sl, blk)
                if j == 3 and n + 1 < 8:
                    build_tables(n + 1)
            rot_rest(*pend)
            pend = None
        S.alias([("vT", 0)], [("Ctab", 1), ("Stab", 1)])
        wv = bf(R5[:, 0:2048]).rearrange("p (k n) -> p k n", k=8)
        S.dma_group("pool", [((lambda e, d=wv[:, k, :], s_=w_in_v[:, k, 1024:1536]: e.dma_start(out=d, in_=s_)), [], ["wv", ("t1", 0), ("t1", 1), ("t2", 0), ("t2", 1)]) for k in range(8)], "wv")
        for j in range(4):
            for n in range(8):
                pi = next_pb()
                MM(pb[pi][:], [(wv[:, k, j * 128:(j + 1) * 128], hT[:, k, n * 512:(n + 1) * 512]) for k in range(8)],
                   ["wv"] + hT_blk(n), [("pb", pi)])
                ACT(vT[:, j, n * 512:(n + 1) * 512], pb[pi][:], AF.Copy, [("pb", pi)], [("vT", j)])
        if dbg == "qkv":
            return finish([(bf(R2[:]), [("qk", j) for j in range(8)]), (bf(R3[:]), [("vT", j) for j in range(4)])])

        S.barrier()
        accS = [R1[:, 6400:10496], R1[:, 10496:14592]]
        aob = bf(R1[:, 0:2048])
        sqa = bf(R1[:, 2048:4096])
        NPT = 3
        pTs = [bf(R1[:, 4096 + i * 256:4096 + (i + 1) * 256]).rearrange("p (h n) -> p h n", h=2) for i in range(NPT)]
        vts = [bf(R1[:, 4864 + i * 128:4864 + (i + 1) * 128]).rearrange("p (h c) -> p h c", h=2) for i in range(NPT)]
        rls = [R1[:, 5248 + i * 512:5248 + (i + 1) * 512] for i in range(2)]
        for i in range(NPT):
            S.op("pool", lambda e, v_=vts[i]: e.memset(v_, 1.0), [], [("vt", i)])
        vtp = [bf(pb[2][:])[:, 0:128], bf(pb[3][:])[:, 0:128]]
        sTv = [pb[i][:].rearrange("p (h n) -> p h n", h=2) for i in range(2)]
        qz = [[bf(R4[:, 0:2048]), bf(R4[:, 2048:4096])], [bf(R4[:, 4096:6144]), bf(R5[:, 0:2048])]]
        for par in range(2):
            S.op("pool", lambda e, t_=qz[par][0][64:128, :]: e.memset(t_, 0.0), [], [("qz", par, 0)])
            S.op("pool", lambda e, t_=qz[par][1][0:64, :]: e.memset(t_, 0.0), [], [("qz", par, 1)])
        w2 = bf(R2[:]).rearrange("p (k n) -> p k n", k=32)

        def load_w2_group(q8):
            fbs = range(q8 * 4, q8 * 4 + 4)
            S.dma_group("pool", [((lambda e, d=w2[:, q8 * 4:q8 * 4 + 4, :], s_=w_ff2_v[:, q8 * 4:q8 * 4 + 4, :]: e.dma_start(out=d, in_=s_)),
                                  [], [("w2", fb) for fb in fbs])], ("w2g", q8))

        gb = [0]
        for ch in range(4):
            par = ch % 2
            if ch >= 1:
                S.alias([("w2", fb) for fb in range((ch - 1) * 4, (ch - 1) * 4 + 4)], [("qk", ch - 1)])
                S.alias([("w2", fb) for fb in range(16 + (ch - 1) * 4, 16 + (ch - 1) * 4 + 4)], [("qk", 4 + ch - 1)])
                load_w2_group(ch - 1)
                load_w2_group(4 + ch - 1)
            CP("dve", qz[par][0][0:64, :], qT[0:64, ch, :], [("qk", ch)], [("qz", par, 0)])
            CP("dve", qz[par][1][64:128, :], qT[64:128, ch, :], [("qk", ch)], [("qz", par, 1)])
            blocks = []
            for p_, d_ in enumerate(DILS):
                n_ = SEQ // d_
                nkb = n_ // 128
                nbank = n_ // 512 if n_ >= 512 else 1
                bankw = min(512, n_)
                for r in range(d_):
                    for kb in range(nkb):
                        q0 = max(0, 128 * kb - 64)
                        q1 = min(n_, 128 * kb + 192)
                        blocks.append(dict(p=p_, d=d_, r=r, kb=kb, q0=q0, q1=q1, N=q1 - q0, moff=q0 - (128 * kb - 64),
                                           nkb=nkb, nbank=nbank, bankw=bankw, gi=gb[0],
                                           ktok=slice(r + d_ * 128 * kb, r + d_ * (128 * kb + 127) + 1, d_),
                                           qtok=slice(r + d_ * q0, r + d_ * (q1 - 1) + 1, d_)))
                        gb[0] += 1
            started = {}

            def stageA(B):
                gi = B["gi"]
                vs, v3, N = gi % 2, gi % NPT, B["N"]
                TRS([(vtp[vs], vT[:, ch, B["ktok"]])], [("vT", ch), "ident_bf"], [("vtp", vs)])
                CP("dve", vts[v3][:, :, 0:64], vtp[vs].rearrange("p (h c) -> p h c", h=2), [("vtp", vs)], [("vt", v3)])
                def fn(e, vs=vs, N=N, B=B, ch=ch, par=par):
                    e.matmul(sTv[vs][:, 0, 0:N], lhsT=kT[:, ch, B["ktok"]], rhs=qz[par][0][:, B["qtok"]], start=True, stop=False, skip_group_check=True)
                    e.matmul(sTv[vs][:, 1, 0:N], lhsT=kT[:, ch, B["ktok"]], rhs=qz[par][1][:, B["qtok"]], start=False, stop=False, skip_group_check=True)
                    e.matmul(sTv[vs][:, 0, 0:N], lhsT=ident_bf[:], rhs=maskb[:, B["moff"]:B["moff"] + N],
                             start=False, stop=False, skip_group_check=True)
                    return e.matmul(sTv[vs][:, 1, 0:N], lhsT=ident_bf[:], rhs=maskb[:, B["moff"]:B["moff"] + N],
                                    start=False, stop=True, skip_group_check=True)
                S.op("pe", fn, [("qk", 4 + ch), ("qz", par, 0), ("qz", par, 1), "maskb", "ident_bf"], [("sT", vs)])

            def stageB(B):
                gi = B["gi"]
                vs, v3, N = gi % 2, gi % NPT, B["N"]
                ACT(pTs[v3][:, :, 0:N], sTv[vs][:, :, 0:N], AF.Exp, [("sT", vs)], [("pT", v3)], scale=0.125)

            def stageC(B):
                gi = B["gi"]
                v3, N, q0, q1, bankw, d_, r = gi % NPT, B["N"], B["q0"], B["q1"], B["bankw"], B["d"], B["r"]
                for hh in range(2):
                    a = q0
                    while a < q1:
                        bk = a // bankw
                        b_end = min(q1, (bk + 1) * bankw)
                        pbi = 4 + hh * 2 + (bk % 2)
                        key = (B["p"], r, hh, bk)
                        first = key not in started
                        started[key] = True
                        MM(pb[pbi][0:65, a - bk * bankw:b_end - bk * bankw],
                           [(vts[v3][:, hh, 0:65], pTs[v3][:, hh, a - q0:b_end - q0])],
                           [("vt", v3), ("pT", v3)], [("accP", pbi)], first_start=first, skip=True)
                        a = b_end
                    for bk in range(B["nbank"]):
                        last_kb = min(B["nkb"] - 1, (bk * bankw + bankw - 1 + 64) // 128)
                        if last_kb == B["kb"]:
                            pbi = 4 + hh * 2 + (bk % 2)
                            tok = slice(r + d_ * bk * bankw, r + d_ * (bk * bankw + bankw - 1) + 1, d_)
                            dstA = accS[hh][0:65, tok]
                            if B["p"] == 0:
                                CP("dve", dstA, pb[pbi][0:65, 0:bankw], [("accP", pbi)], [("accS", hh)])
                            else:
                                TT("dve", dstA, pb[pbi][0:65, 0:bankw], dstA, ALU.add, [("accP", pbi), ("accS", hh)], [("accS", hh)])

            stageA(blocks[0])
            for i_, B in enumerate(blocks):
                if i_ + 1 < len(blocks):
                    stageA(blocks[i_ + 1])
                stageB(B)
                stageC(B)
            for hh in range(2):
                for n in range(8):
                    blk = slice(n * 512, (n + 1) * 512)
                    rs = (hh * 8 + n) % 2
                    MM(pb[2 + rs][0:64, :], [(SEL, accS[hh][0:65, blk])], ["cstf", ("accS", hh)], [("vtp", rs)])
                    ACT(rls[rs][0:64, :], pb[2 + rs][0:64, :], AF.Ln, [("vtp", rs)], [("rl", rs)])
                    ACT(rls[rs][0:64, :], rls[rs][0:64, :], AF.Exp, [("rl", rs)], [("rl", rs)], scale=-1.0)
                    TT("dve", aob[64 * hh:64 * hh + 64, blk], accS[hh][0:64, blk], rls[rs][0:64, :], ALU.mult,
                       [("accS", hh), ("rl", rs)], [("aob", hh)])
            ACT(sqa, aob, AF.Square, [("aob", 0), ("aob", 1)], ["sqa"])

            def fn(e):
                ins = None
                for tt in range(NT):
                    ins = e.matmul(pb[3][:, tt:tt + 1], lhsT=sqa[:, tt * 128:(tt + 1) * 128], rhs=ones_bf[:, 0:1], start=True, stop=True)
                return ins
            S.op("pe", fn, ["sqa", "ones_bf"], [("vtp", 1)])
            if ch == 0:
                CP("dve", ssq_a[:], pb[3][:, 0:NT], [("vtp", 1)], ["ssq_a"])
            else:
                TT("dve", ssq_a[:], pb[3][:, 0:NT], ssq_a[:], ALU.add, [("vtp", 1), "ssq_a"], ["ssq_a"])
            DMA("sp", mixs[ch, :, :], aob, "aost", [("aob", 0), ("aob", 1)], [])
        if dbg == "attn":
            return finish([(ssq_a[:], ["ssq_a"]), (ssq_l[:], ["ssq_l"])])

        S.barrier()
        w1 = bf(R1[:]).rearrange("p (k n) -> p k n", k=8)
        wo = bf(R3[:, 0:4096]).rearrange("p (k n) -> p k n", k=8)
        for q8 in range(8):
            fbs = range(q8 * 4, q8 * 4 + 4)
            S.dma_group("pool", [((lambda e, d=w1[:, :, q8 * 512:(q8 + 1) * 512], s_=w_ff1_v[:, :, q8 * 512:(q8 + 1) * 512]: e.dma_start(out=d, in_=s_)),
                                  [], [("w1", fb) for fb in fbs])], ("w1g", q8))
            if q8 in (3, 7):
                load_w2_group(q8)
        NW = 4
        wtmp = [R4[:, i * 1024:(i + 1) * 1024] for i in range(NW)]
        g1_bc = R4[:, 4096:5120]
        g2_bc = R4[:, 5120:6144]
        bscr2 = R4[:, 6144:7168]
        bcast_rows(g1_bc, modc[:, 16:24], "modc", "g1_bc", bscr2)
        bcast_rows(g2_bc, modc[:, 40:48], "modc", "g2_bc", bscr2)
        bcast_rows(fg_bc, pcolA[:, 72:80], "pcolA", "fg_bc", bscr2)
        for ssq_, rstd_, tok in ((ssq_a, rstd_a, "rstd_a"), (ssq_l, rstd_l, "rstd_l")):
            TS("dve", ms1[:], ssq_[:], 1.0 / 512.0, EPS, ALU.mult, ALU.add, ["ssq_a", "ssq_l", "ms1"], ["ms1"])
            TT("pool", rstd_[:], ms1[:], neghalf32[:], ALU.pow, ["ms1", "neghalf32"], [tok])
        for k in range(8):
            s_ = k % NW
            DMA("sp", wtmp[s_], w_out_v[:, k, :], ("wtmp", s_), [], [("wtmp", s_)])
            STT(wo[:, k, :], wtmp[s_], pcolB[:, 44 + k:45 + k], g1_bc, ALU.mult, ALU.mult, [("wtmp", s_), "pcolB", "g1_bc"], ["wo"])
        S.alias(["mg"], [("wtmp", 0)])
        S.alias([("pb", 4)], [("pbb", k) for k in range(4, 8)])
        S.alias([("acc", 1, 1)], [("pbb", k) for k in range(4)])
        S.alias([("x1g", 0, 0), ("x1g", 0, 1)], [("wtmp", 1), ("wtmp", 2)])
        S.alias([("x1g", 1, 0), ("x1g", 1, 1)], [("wtmp", 3), "g1_bc"])
        G = 256
        NG = SEQ // G
        mg = bf(R4[:, 0:1024]).rearrange("p (k t) -> p k t", k=8)
        x1g = [R4[:, 1024 + i * 2048:1024 + (i + 1) * 2048].rearrange("p (t f) -> p t f", t=2) for i in range(2)]
        h2gs = [bf(R3[:, 4096 + i * 1024:4096 + (i + 1) * 1024]).rearrange("p (k t) -> p k t", k=8) for i in range(2)]
        hidb = [bf(R3[:, 6144 + i * 128:6144 + (i + 1) * 128]) for i in range(4)]
        tmpf = [R3[:, 6656 + i * 512:6656 + (i + 1) * 512] for i in range(2)] + [R4[:, 6144 + i * 512:6144 + (i + 1) * 512] for i in range(2)]
        S.alias([("tmpf", 2), ("tmpf", 3)], [("bcs", k) for k in range(8)])
        xs2 = [bf(R3[:, 7680:8192]), bf(R5[:, 192:704])]
        ssq2 = R5[:, 0:32]
        ms2 = R5[:, 32:64]
        rstd2 = R5[:, 64:96]
        ssq3 = R5[:, 96:128]
        ms3 = R5[:, 128:160]
        rstd3 = R5[:, 160:192]
        junk2 = bf(R5[:, 704:1216])
        hid = [R5[:, 2240 + i * 256:2240 + (i + 1) * 256] for i in range(3)]

        def pro_dma(g):
            DMA("sp", mg, mixs_v[:, :, g * G:(g + 1) * G], "mg", [], ["mg"])
            DMA("sp", x1g[g % 2], x[g * G:(g + 1) * G, :].rearrange("(t p) f -> p t f", p=128), ("xg", g % 2), [],
                [("x1g", g % 2, 0), ("x1g", g % 2, 1)])

        def pro_woutA(g, tt, hf):
            tile_i = g * 2 + tt
            cs_ = slice(hf * 512, (hf + 1) * 512)
            ts_ = (tt * 2 + hf) % 2
            MM(pb[4][:], [(mg[:, k, tt * 128:(tt + 1) * 128], wo[:, k, cs_]) for k in range(4)], ["mg", "wo"], [("pb", 4)])
            TS("dve", tmpf[ts_], pb[4][:], rstd_a[:, tile_i:tile_i + 1], None, ALU.mult, None, [("pb", 4), "rstd_a"], [("tmpf", ts_)])

        def pro_woutL(g, tt, hf):
            xb_ = g % 2
            tile_i = g * 2 + tt
            cs_ = slice(hf * 512, (hf + 1) * 512)
            ts_ = (tt * 2 + hf) % 2
            MM(pb[4][:], [(mg[:, k, tt * 128:(tt + 1) * 128], wo[:, k, cs_]) for k in range(4, 8)], ["mg", "wo"], [("pb", 4)])
            STT(tmpf[ts_], pb[4][:], rstd_l[:, tile_i:tile_i + 1], tmpf[ts_], ALU.mult, ALU.add, [("pb", 4), "rstd_l", ("tmpf", ts_)], [("tmpf", ts_)])
            TT("dve", x1g[xb_][:, tt, cs_], tmpf[ts_], x1g[xb_][:, tt, cs_], ALU.add, [("tmpf", ts_), ("x1g", xb_, tt)], [("x1g", xb_, tt)])

        def pro_norm(g, tt):
            xb_ = g % 2
            tile_i = g * 2 + tt
            tok = ("n2", tile_i)
            ACT(junk2, x1g[xb_][:, tt, :], AF.Square, [("x1g", xb_, tt)], ["junk2", ("ssq", tok)], accum=ssq2[:, tile_i:tile_i + 1])
            TS("dve", ms2[:, tile_i:tile_i + 1], ssq2[:, tile_i:tile_i + 1], 1.0 / DM, EPS, ALU.mult, ALU.add, [("ssq", tok)], [("ms", tok)])
            TT("pool", rstd2[:, tile_i:tile_i + 1], ms2[:, tile_i:tile_i + 1], NEGHALF, ALU.pow, [("ms", tok), "cstf"], [("rstd", tok)])
            TS("dve", xs2[tt], x1g[xb_][:, tt, :], rstd2[:, tile_i:tile_i + 1], None, ALU.mult, None, [("rstd", tok), ("x1g", xb_, tt)], [("xs2", tt)])

        def pro_tr(g, tt):
            h2g = h2gs[g % 2]
            pv = bf(pb[4][:]).rearrange("p (k t) -> p k t", k=8)
            TRS([(pv[:, k, :], xs2[tt][:, k * 128:(k + 1) * 128]) for k in range(8)], [("xs2", tt), "ident_bf"], [("pb", 4)])
            for k in range(8):
                TS("dve", h2g[:, k, tt * 128:(tt + 1) * 128], pv[:, k, :], gs2[:, k:k + 1], SH2[:, k:k + 1], ALU.mult, ALU.add,
                   [("pb", 4), "gs2", "modc"], [("h2g", g % 2, tt)])

        def ff1(g, fb):
            h2g = h2gs[g % 2]
            hs = fb % 3
            MM(pb[5 + hs][:, 0:256], [(w1[:, k, fb * 128:(fb + 1) * 128], h2g[:, k, :]) for k in range(8)],
               [("w1", fb), ("h2g", g % 2, 0), ("h2g", g % 2, 1)], [("pb", 5 + hs)])
            ACT(hid[hs], pb[5 + hs][:, 0:256], AF.Relu, [("pb", 5 + hs)], [("hid", hs)])
            TT("pool", hidb[fb % 4], hid[hs], hid[hs], ALU.mult, [("hid", hs)], [("hidb", fb % 4)])

        def ff2(g, fb):
            for tt in range(2):
                for hf in range(2):
                    MM(pb[tt * 2 + hf][:], [(hidb[fb % 4][:, tt * 128:(tt + 1) * 128], w2[:, fb, hf * 512:(hf + 1) * 512])],
                       [("hidb", fb % 4), ("w2", fb)], [("acc", tt, hf)], first_start=(fb == 0), skip=True)

        def epilogue(g):
            xb_ = g % 2
            for tt in range(2):
                tile_i = g * 2 + tt
                for hf in range(2):
                    cs_ = slice(hf * 512, (hf + 1) * 512)
                    ts_ = tt * 2 + hf
                    TT("dve", tmpf[ts_], pb[tt * 2 + hf][:], g2_bc[:, cs_], ALU.mult, [("acc", tt, hf), "g2_bc"], [("tmpf", ts_)])
            for tt in range(2):
                for hf in range(2):
                    cs_ = slice(hf * 512, (hf + 1) * 512)
                    ts_ = tt * 2 + hf
                    TT("dve", x1g[xb_][:, tt, cs_], tmpf[ts_], x1g[xb_][:, tt, cs_], ALU.add,
                       [("tmpf", ts_), ("x1g", xb_, tt)], [("x1g", xb_, tt)])
            for tt in range(2):
                tile_i = g * 2 + tt
                tok = ("n3", tile_i)
                ACT(junk2, x1g[xb_][:, tt, :], AF.Square, [("x1g", xb_, tt)], ["junk2", ("ssq", tok)], accum=ssq3[:, tile_i:tile_i + 1])
                TS("dve", ms3[:, tile_i:tile_i + 1], ssq3[:, tile_i:tile_i + 1], 1.0 / DM, EPS, ALU.mult, ALU.add, [("ssq", tok)], [("ms", tok)])
                TT("pool", rstd3[:, tile_i:tile_i + 1], ms3[:, tile_i:tile_i + 1], NEGHALF, ALU.pow, [("ms", tok), "cstf"], [("rstd", tok)])
                STT(x1g[xb_][:, tt, :], x1g[xb_][:, tt, :], rstd3[:, tile_i:tile_i + 1], fg_bc[:], ALU.mult, ALU.mult,
                    [("x1g", xb_, tt), ("rstd", tok), "fg_bc"], [("x1g", xb_, tt)])
                DMA("sp", y[tile_i * 128:(tile_i + 1) * 128, :], x1g[xb_][:, tt, :], ("yst", xb_), [("x1g", xb_, tt)], [])

        sched = {0: [("dma",)], 2: [("woutA", 0, 0)], 3: [("woutL", 0, 0)], 4: [("woutA", 0, 1)], 5: [("woutL", 0, 1)], 7: [("norm", 0)],
                 8: [("woutA", 1, 0)], 9: [("woutL", 1, 0)], 10: [("woutA", 1, 1)], 11: [("woutL", 1, 1)],
                 13: [("norm", 1)], 17: [("tr", 0)], 23: [("tr", 1)]}

        def run_pro(g, item):
            if item[0] == "dma":
                pro_dma(g)
            elif item[0] == "woutA":
                pro_woutA(g, item[1], item[2])
            elif item[0] == "woutL":
                pro_woutL(g, item[1], item[2])
            elif item[0] == "norm":
                pro_norm(g, item[1])
            else:
                pro_tr(g, item[1])

        for fbk in sorted(sched):
            for item in sched[fbk]:
                run_pro(0, item)
        for g in range(NG):
            ff1(g, 0)
            ff1(g, 1)
            for fb in range(32):
                if fb + 2 < 32:
                    ff1(g, fb + 2)
                ff2(g, fb)
                if g + 1 < NG:
                    for item in sched.get(fb, ()):
                        run_pro(g + 1, item)
            epilogue(g)
        return finish()


def make_consts():
    c = np.zeros((128, 512), np.float32)
    kk = np.arange(128)[:, None]
    qq = np.arange(256)[None, :]
    c[:, 0:256] = np.where((qq >= kk) & (qq <= kk + 128), 0.0, -240000.0)
    psw = np.zeros((128, 128), np.float32)
    invf = np.zeros(128, np.float32)
    sgn = np.zeros(128, np.float32)
    for hb in (0, 64):
        for j in range(8):
            psw[hb + j + 8, hb + j] = 1.0
            psw[hb + j, hb + j + 8] = 1.0
            f = np.float32(500000.0) ** (-np.float32(j) / np.float32(8))
            invf[hb + j] = f
            invf[hb + j + 8] = f
            sgn[hb + j] = -1.0
            sgn[hb + j + 8] = 1.0
    c[:, 256:384] = psw
    c[64, 384:448] = 1.0
    c[:, 448] = invf
    c[:, 449] = sgn
    c[:, 450] = -0.5
    return c


_CACHE = {}


def make_in_maps(inputs):
    f = lambda a: np.ascontiguousarray(np.asarray(a))
    shared = {
        "w_ada": f(inputs["w_ada"][0]), "b_ada": f(inputs["b_ada"][0]).reshape(48, 128),
        "norm1_g": f(inputs["norm1_g"][0]).reshape(8, 128), "norm2_g": f(inputs["norm2_g"][0]).reshape(8, 128),
        "final_g": f(inputs["final_g"]).reshape(8, 128), "w_in": f(inputs["w_in"][0]),
        "conv_w": f(inputs["conv_w"][0]).reshape(16, 128), "conv_b": f(inputs["conv_b"][0]).reshape(4, 128),
        "lru_wa": f(inputs["lru_wa"][0]).reshape(16, 64, 64), "lru_wx": f(inputs["lru_wx"][0]).reshape(16, 64, 64),
        "lru_ba": f(inputs["lru_ba"][0]).reshape(8, 128), "lru_bx": f(inputs["lru_bx"][0]).reshape(8, 128),
        "lru_lam": f(inputs["lru_lam"][0]).reshape(8, 128), "attn_out_g": f(inputs["attn_out_g"][0]).reshape(4, 128),
        "lru_out_g": f(inputs["lru_out_g"][0]).reshape(4, 128), "w_out": f(inputs["w_out"][0]),
        "w_ff1": f(inputs["w_ff1"][0]), "w_ff2": f(inputs["w_ff2"][0]), "cst": make_consts(),
    }
    maps = []
    for b in range(8):
        m = dict(shared)
        m["x"] = f(inputs["x"][b])
        m["crow"] = f(inputs["c"][b]).reshape(8, 128)
        m["pos"] = f(inputs["positions"][b]).reshape(1, SEQ).astype(np.int32)
        maps.append(m)
    return maps


def kernel(**inputs):
    if "nc" not in _CACHE:
        _CACHE["nc"] = build_program()
    nc = _CACHE["nc"]
    in_maps = make_in_maps(inputs)
    res = run_bass_kernel_spmd(nc, in_maps, core_ids=list(range(8)))
    return np.stack([np.asarray(r["y"]) for r in res.results], axis=0).astype(np.float32)
```
